# Optimizing a Trainium2 kernel written in Bass

```python
import math
import jax
import jax.numpy as jnp
from jax import lax
import numpy as np

D_MODEL = 2048
BATCH = 4
SEQ = 4096
DEPTH = 2

CHUNK = 64
N_EVEN = (DEPTH + 1) // 2
N_ODD = DEPTH // 2
NORM_EPS = 1e-6

RWKV_WIDTH = D_MODEL // 2
RWKV_HEAD = 64
RWKV_HEADS = RWKV_WIDTH // RWKV_HEAD
DECAY_LORA = max(32, int(round(1.8 * D_MODEL ** 0.5 / 32)) * 32)
AAA_LORA = max(32, int(round(1.8 * D_MODEL ** 0.5 / 32)) * 32)
GATE_LORA = max(32, int(round(0.6 * D_MODEL ** 0.8 / 32)) * 32)
RWKV_LN_EPS = 64e-5
RWKV_PROJ = 3 * RWKV_WIDTH + DECAY_LORA + AAA_LORA + GATE_LORA

SSM_WIDTH = D_MODEL // 2
SSM_HEAD = 64
SSM_HEADS = SSM_WIDTH // SSM_HEAD
SSM_STATE = 128
SSM_GROUPS = 2
SSM_CONV = 4
SSM_CONV_CH = SSM_WIDTH + 2 * SSM_GROUPS * SSM_STATE
SSM_PROJ = SSM_WIDTH + SSM_CONV_CH + SSM_HEADS
IN_AB = RWKV_PROJ + SSM_PROJ
MIX_AB = RWKV_WIDTH + SSM_WIDTH

ATT_HEADS = 16
ATT_HEAD = D_MODEL // ATT_HEADS
LEFT_CHUNKS = 8
BAND = (LEFT_CHUNKS + 1) * CHUNK
REL_PAST_CLIP = 256
REL_FUTURE = CHUNK - 1
REL_BUCKETS = REL_PAST_CLIP + REL_FUTURE + 1

D_FF = -(-8 * D_MODEL // (3 * 256)) * 256

kernel_name = 'hybrid_rwkv7_ssd_chunkattn_encoder'


def rms_norm(x, g):
    xf = x.astype(jnp.float32)
    y = xf * lax.rsqrt(jnp.mean(xf * xf, axis=-1, keepdims=True) + NORM_EPS)
    return (y * g.astype(jnp.float32)).astype(x.dtype)


def token_shift(x):
    return jnp.pad(x, ((0, 0), (1, 0), (0, 0)))[:, :-1]


def causal_dwconv(x, w, b):
    K, C = w.shape
    y = lax.conv_general_dilated(x, w[:, None, :], window_strides=(1,), padding=[(K - 1, 0)],
                                 dimension_numbers=('NWC', 'WIO', 'NWC'), feature_group_count=C)
    return y + b


def rwkv7_time_mix(p, mu, w0, w2, a0, a2, g2, k_k, k_a, r_k, ln_w, ln_b):
    f32 = jnp.float32
    Bsz, T, _ = p.shape
    p = p.astype(f32)
    p = p + (token_shift(p) - p) * mu
    s1 = RWKV_WIDTH
    s4 = 3 * s1 + DECAY_LORA
    s5 = s4 + AAA_LORA
    r, k, v, wc, ac, gc = jnp.split(p, [s1, 2 * s1, 3 * s1, s4, s5], axis=-1)
    w = -jax.nn.softplus(-(w0 + jnp.tanh(wc) @ w2)) - 0.5
    decay = jnp.exp(-jnp.exp(w))
    a = jax.nn.sigmoid(a0 + ac @ a2)
    g = jax.nn.sigmoid(gc) @ g2

    def heads(t):
        return t.reshape(Bsz, T, RWKV_HEADS, RWKV_HEAD)

    kk = heads(k * k_k)
    kk = kk * lax.rsqrt(jnp.maximum(jnp.sum(kk * kk, axis=-1, keepdims=True), 1e-24))
    k = k * (1.0 + (a - 1.0) * k_a)
    a_vec = -kk
    b_vec = kk * heads(a)

    def step(S, inp):
        r_t, w_t, k_t, v_t, a_t, b_t = inp
        sa = jnp.einsum('bhvk,bhk->bhv', S, a_t)
        S = S * w_t[:, :, None, :] + sa[..., None] * b_t[:, :, None, :] + v_t[..., None] * k_t[:, :, None, :]
        return S, jnp.einsum('bhvk,bhk->bhv', S, r_t)

    xs = tuple(jnp.swapaxes(t, 0, 1) for t in (heads(r), heads(decay), heads(k), heads(v), a_vec, b_vec))
    S0 = jnp.zeros((Bsz, RWKV_HEADS, RWKV_HEAD, RWKV_HEAD), f32)
    _, y = lax.scan(step, S0, xs)
    y = jnp.swapaxes(y, 0, 1)
    mean = jnp.mean(y, axis=-1, keepdims=True)
    var = jnp.mean(jnp.square(y - mean), axis=-1, keepdims=True)
    y = ((y - mean) * lax.rsqrt(var + RWKV_LN_EPS)).reshape(Bsz, T, RWKV_WIDTH) * ln_w + ln_b
    bonus = jnp.sum(heads(r) * heads(k) * r_k, axis=-1, keepdims=True) * heads(v)
    return (y + bonus.reshape(Bsz, T, RWKV_WIDTH)) * g


def ssd_chunked(x, dt, A, Bm, Cm):
    Bsz, T, H, P = x.shape
    nc = T // CHUNK
    hpg = H // SSM_GROUPS
    Xc = (x * dt[..., None]).reshape(Bsz, nc, CHUNK, H, P)
    Ac = jnp.transpose((A * dt).reshape(Bsz, nc, CHUNK, H), (0, 3, 1, 2))
    Bh = jnp.repeat(Bm, hpg, axis=2).reshape(Bsz, nc, CHUNK, H, SSM_STATE)
    Ch = jnp.repeat(Cm, hpg, axis=2).reshape(Bsz, nc, CHUNK, H, SSM_STATE)
    A_cs = jnp.cumsum(Ac, axis=-1)
    causal = jnp.asarray(np.tril(np.ones((CHUNK, CHUNK), dtype=bool)))
    seg = A_cs[..., :, None] - A_cs[..., None, :]
    L = jnp.where(causal, jnp.exp(jnp.where(causal, seg, 0.0)), 0.0)
    scores = jnp.einsum('bclhn,bcshn->bhcls', Ch, Bh) * L
    y_diag = jnp.einsum('bhcls,bcshp->bclhp', scores, Xc)
    decay_states = jnp.exp(A_cs[..., -1:] - A_cs)
    states = jnp.einsum('bclhn,bhcl,bclhp->bchpn', Bh, decay_states, Xc)
    chunk_decay = jnp.exp(A_cs[..., -1])

    def step(hstate, inp):
        s_c, d_c = inp
        return hstate * d_c[..., None, None] + s_c, hstate

    h0 = jnp.zeros((Bsz, H, P, SSM_STATE), x.dtype)
    _, prev = lax.scan(step, h0, (jnp.swapaxes(states, 0, 1), jnp.transpose(chunk_decay, (2, 0, 1))))
    prev = jnp.swapaxes(prev, 0, 1)
    y_off = jnp.einsum('bclhn,bchpn,bhcl->bclhp', Ch, prev, jnp.exp(A_cs))
    return (y_diag + y_off).reshape(Bsz, T, H, P)


def mamba2_mix(p, conv_w, conv_b, dt_bias, A_log, D_skip, norm_w):
    f32 = jnp.float32
    Bsz, T, _ = p.shape
    p = p.astype(f32)
    z, xBC, dt = jnp.split(p, [SSM_WIDTH, SSM_WIDTH + SSM_CONV_CH], axis=-1)
    xBC = jax.nn.silu(causal_dwconv(xBC, conv_w.astype(f32), conv_b.astype(f32)))
    xs, Bm, Cm = jnp.split(xBC, [SSM_WIDTH, SSM_WIDTH + SSM_GROUPS * SSM_STATE], axis=-1)
    dt = jax.nn.softplus(dt + dt_bias)
    A = -jnp.exp(A_log.astype(f32))
    xh = xs.reshape(Bsz, T, SSM_HEADS, SSM_HEAD)
    y = ssd_chunked(xh, dt, A, Bm.reshape(Bsz, T, SSM_GROUPS, SSM_STATE),
                    Cm.reshape(Bsz, T, SSM_GROUPS, SSM_STATE))
    y = (y + D_skip[:, None] * xh).reshape(Bsz, T, SSM_WIDTH) * jax.nn.silu(z)
    yg = y.reshape(Bsz, T, SSM_GROUPS, SSM_WIDTH // SSM_GROUPS)
    yg = yg * lax.rsqrt(jnp.mean(yg * yg, axis=-1, keepdims=True) + NORM_EPS)
    return yg.reshape(Bsz, T, SSM_WIDTH) * norm_w


def parallel_ab_mixer(h, w_in, mu, w0, w2, a0, a2, g2, k_k, k_a, r_k, ln_w, ln_b,
                      conv_w, conv_b, dt_bias, A_log, D_skip, norm_w, w_out):
    p = h @ w_in
    p_rwkv, p_ssm = jnp.split(p, [RWKV_PROJ], axis=-1)
    y_a = rwkv7_time_mix(p_rwkv, mu, w0, w2, a0, a2, g2, k_k, k_a, r_k, ln_w, ln_b)
    y_b = mamba2_mix(p_ssm, conv_w, conv_b, dt_bias, A_log, D_skip, norm_w)
    y = jnp.concatenate([y_a, y_b], axis=-1).astype(h.dtype)
    return y @ w_out


def chunk_band_attention(h, w_qkv, rel_bias, w_out):
    Bsz, T, _ = h.shape
    nc = T // CHUNK
    pad = LEFT_CHUNKS * CHUNK
    q, k, v = jnp.split(h @ w_qkv, 3, axis=-1)
    q = q.reshape(Bsz, T, ATT_HEADS, ATT_HEAD)
    k = jnp.pad(k.reshape(Bsz, T, ATT_HEADS, ATT_HEAD), ((0, 0), (pad, 0), (0, 0), (0, 0)))
    v = jnp.pad(v.reshape(Bsz, T, ATT_HEADS, ATT_HEAD), ((0, 0), (pad, 0), (0, 0), (0, 0)))
    rel = (jnp.arange(CHUNK)[:, None] + pad) - jnp.arange(BAND)[None, :]
    rel = jnp.clip(rel, -REL_FUTURE, REL_PAST_CLIP) + REL_FUTURE
    bias = rel_bias[:, rel].astype(jnp.float32)
    scale = ATT_HEAD ** -0.5
    q_chunks = jnp.swapaxes(q.reshape(Bsz, nc, CHUNK, ATT_HEADS, ATT_HEAD), 0, 1)

    def one_chunk(args):
        q_c, c = args
        start = c * CHUNK
        k_b = lax.dynamic_slice_in_dim(k, start, BAND, axis=1)
        v_b = lax.dynamic_slice_in_dim(v, start, BAND, axis=1)
        s = jnp.einsum('bqhd,bkhd->bhqk', q_c, k_b, preferred_element_type=jnp.float32) * scale + bias
        valid = (start - pad + jnp.arange(BAND)) >= 0
        s = jnp.where(valid, s, -1e30)
        pr = jax.nn.softmax(s, axis=-1)
        return jnp.einsum('bhqk,bkhd->bqhd', pr.astype(v_b.dtype), v_b)

    out = lax.map(one_chunk, (q_chunks, jnp.arange(nc)))
    out = jnp.swapaxes(out, 0, 1).reshape(Bsz, T, D_MODEL)
    return out @ w_out


def swiglu(h, w_gate, w_up, w_down):
    return (jax.nn.silu(h @ w_gate) * (h @ w_up)) @ w_down


def setup_inputs(seed: int = 0) -> dict:
    key = jax.random.key(seed)
    ks = jax.random.split(key, 32)
    f32 = jnp.float32

    def nrm(i, shape, s):
        return jax.random.normal(ks[i], shape, f32) * s

    def unif(i, shape, lo, hi):
        return jax.random.uniform(ks[i], shape, f32, lo, hi)

    E, O = N_EVEN, N_ODD
    dt0 = jnp.exp(unif(15, (E, SSM_HEADS), math.log(1e-3), math.log(1e-1)))
    return {
        'x': nrm(0, (BATCH, SEQ, D_MODEL), 1.0),
        'norm_g': 1.0 + nrm(1, (DEPTH, 4, D_MODEL), 0.02),
        'w_in_ab': nrm(2, (E, D_MODEL, IN_AB), D_MODEL ** -0.5),
        'rwkv_mu': unif(3, (E, RWKV_PROJ), 0.0, 1.0),
        'rwkv_w0': unif(4, (E, RWKV_WIDTH), -6.0, -1.0),
        'rwkv_w2': nrm(5, (E, DECAY_LORA, RWKV_WIDTH), DECAY_LORA ** -0.5),
        'rwkv_a0': nrm(6, (E, RWKV_WIDTH), 0.1),
        'rwkv_a2': nrm(7, (E, AAA_LORA, RWKV_WIDTH), AAA_LORA ** -0.5),
        'rwkv_g2': nrm(8, (E, GATE_LORA, RWKV_WIDTH), GATE_LORA ** -0.5),
        'rwkv_k_k': 0.85 + nrm(9, (E, RWKV_WIDTH), 0.02),
        'rwkv_k_a': 1.0 + nrm(10, (E, RWKV_WIDTH), 0.02),
        'rwkv_r_k': nrm(11, (E, RWKV_HEADS, RWKV_HEAD), 0.1),
        'rwkv_ln_w': 1.0 + nrm(12, (E, RWKV_WIDTH), 0.02),
        'rwkv_ln_b': nrm(13, (E, RWKV_WIDTH), 0.02),
        'ssm_conv_w': nrm(14, (E, SSM_CONV, SSM_CONV_CH), SSM_CONV ** -0.5),
        'ssm_conv_b': nrm(16, (E, SSM_CONV_CH), 0.02),
        'ssm_dt_bias': dt0 + jnp.log(-jnp.expm1(-dt0)),
        'ssm_A_log': jnp.log(unif(17, (E, SSM_HEADS), 1.0, 16.0)),
        'ssm_D': 1.0 + nrm(18, (E, SSM_HEADS), 0.1),
        'ssm_norm_w': 1.0 + nrm(19, (E, SSM_WIDTH), 0.02),
        'w_out_ab': nrm(20, (E, MIX_AB, D_MODEL), MIX_AB ** -0.5),
        'w_qkv': nrm(21, (O, D_MODEL, 3 * D_MODEL), D_MODEL ** -0.5),
        'attn_rel_bias': nrm(22, (O, ATT_HEADS, REL_BUCKETS), 0.2),
        'w_out_c': nrm(23, (O, D_MODEL, D_MODEL), D_MODEL ** -0.5),
        'ffn_w_gate': nrm(24, (DEPTH, D_MODEL, D_FF), D_MODEL ** -0.5),
        'ffn_w_up': nrm(25, (DEPTH, D_MODEL, D_FF), D_MODEL ** -0.5),
        'ffn_w_down': nrm(26, (DEPTH, D_FF, D_MODEL), D_FF ** -0.5),
    }


def reference(x, norm_g, w_in_ab, rwkv_mu, rwkv_w0, rwkv_w2, rwkv_a0, rwkv_a2, rwkv_g2,
              rwkv_k_k, rwkv_k_a, rwkv_r_k, rwkv_ln_w, rwkv_ln_b, ssm_conv_w, ssm_conv_b,
              ssm_dt_bias, ssm_A_log, ssm_D, ssm_norm_w, w_out_ab, w_qkv, attn_rel_bias,
              w_out_c, ffn_w_gate, ffn_w_up, ffn_w_down):
    for l in range(DEPTH):
        i = l // 2
        h = rms_norm(x, norm_g[l, 0])
        if l % 2 == 0:
            m = parallel_ab_mixer(h, w_in_ab[i], rwkv_mu[i], rwkv_w0[i], rwkv_w2[i], rwkv_a0[i],
                                  rwkv_a2[i], rwkv_g2[i], rwkv_k_k[i], rwkv_k_a[i], rwkv_r_k[i],
                                  rwkv_ln_w[i], rwkv_ln_b[i], ssm_conv_w[i], ssm_conv_b[i],
                                  ssm_dt_bias[i], ssm_A_log[i], ssm_D[i], ssm_norm_w[i], w_out_ab[i])
        else:
            m = chunk_band_attention(h, w_qkv[i], attn_rel_bias[i], w_out_c[i])
        x = x + rms_norm(m, norm_g[l, 1])
        f = swiglu(rms_norm(x, norm_g[l, 2]), ffn_w_gate[l], ffn_w_up[l], ffn_w_down[l])
        x = x + rms_norm(f, norm_g[l, 3])
    return x
```

```python
import contextlib
import numpy as np
import concourse.bass as bass
import concourse.mybir as mybir

F32 = mybir.dt.float32
BF16 = mybir.dt.bfloat16
AF = mybir.ActivationFunctionType
ALU = mybir.AluOpType
AX = mybir.AxisListType


class Buf:
    def __init__(self, ctx, name, shape, dtype, space="sbuf"):
        self.ctx = ctx
        self.name = name
        self.shape = shape
        self.dtype = dtype
        name = ctx.prefix + name
        self.name = name
        if space == "sbuf":
            cm = ctx.nc.sbuf_tensor(name, list(shape), dtype)
        else:
            cm = ctx.nc.psum_tensor(name, list(shape), dtype)
        self.t = ctx.pstack.enter_context(cm)
        ctx.phase_bufs.append(self)
        self.space = space
        self.writes = {}
        self.reads = {}
        self._dsem = None
        self._rsem = None
        self.dcount = 0
        self.rcount = 0

    def __getitem__(self, idx):
        return self.t[idx]

    def dsem(self):
        if self._dsem is None:
            self._dsem, self.dcount = self.ctx.pool_sem()
        return self._dsem

    def rsem(self):
        if self._rsem is None:
            self._rsem, self.rcount = self.ctx.pool_sem()
        return self._rsem


class Ctx:
    ENG = ("pe", "dve", "act", "pool", "sp")

    def __init__(self, nc):
        self.nc = nc
        self.stack = contextlib.ExitStack()
        self.pstack = contextlib.ExitStack()
        self.phase_bufs = []
        self.prefix = ""
        self.free_sems = []
        self.dma_sems = {}
        self.nsem = 0
        self.eng = {"pe": nc.tensor, "dve": nc.vector, "act": nc.scalar,
                    "pool": nc.gpsimd, "sp": nc.sync}
        self.sems = {}
        self.semobj = {}
        self.cnt = {}
        self.known = {e: {} for e in self.ENG}
        for e in self.ENG:
            self.semobj[e] = self.stack.enter_context(nc.semaphore("e_" + e))
            self.cnt[e] = 0
        self.nwaits = 0
        self.nops = 0
        import os
        self.limit = int(os.environ.get("MK_LIMIT", "1000000000"))
        self.log = []

    def new_sem(self, name):
        s = self.stack.enter_context(self.nc.semaphore(name))
        self.semobj[name] = s
        return name

    def pool_sem(self):
        if self.free_sems:
            return self.free_sems.pop()
        self.nsem += 1
        return self.new_sem(f"dq{self.nsem}"), 0

    def barrier(self):
        for b in self.phase_bufs:
            if b._dsem is not None:
                self.dma_sems[b._dsem] = max(self.dma_sems.get(b._dsem, 0), b.dcount)
            if b._rsem is not None:
                self.dma_sems[b._rsem] = max(self.dma_sems.get(b._rsem, 0), b.rcount)
        deps = [(e, self.cnt[e]) for e in self.ENG if self.cnt[e] > 0] + list(self.dma_sems.items())
        for e in self.ENG:
            self._wait(e, deps)

    def end_phase(self, prefix=""):
        self.barrier()
        for b in self.phase_bufs:
            if b._dsem is not None:
                self.free_sems.append((b._dsem, b.dcount))
            if b._rsem is not None:
                self.free_sems.append((b._rsem, b.rcount))
        self.phase_bufs = []
        self.pstack.close()
        self.pstack = contextlib.ExitStack()
        self.prefix = prefix

    def close(self):
        self.pstack.close()
        self.stack.close()

    def sbuf(self, name, shape, dtype):
        return Buf(self, name, shape, dtype, "sbuf")

    def psum(self, name, shape, dtype=F32):
        return Buf(self, name, shape, dtype, "psum")

    def _wait(self, e, deps):
        k = self.known[e]
        need = {}
        for (s, v) in deps:
            if k.get(s, 0) < v and need.get(s, 0) < v:
                need[s] = v
        for s, v in need.items():
            self.eng[e].wait_ge(self.semobj[s], v)
            k[s] = v
            self.nwaits += 1

    def _deps(self, reads, writes):
        deps = []
        for b in reads:
            deps += list(b.writes.items())
            if b.space == "psum":
                deps += list(b.reads.items())
        for b in writes:
            deps += list(b.writes.items())
            deps += list(b.reads.items())
        return deps

    def op(self, e, fn, reads=(), writes=()):
        if self.nops >= self.limit:
            return None
        import traceback
        self.log.append((self.nops, e, traceback.extract_stack(limit=3)[0].lineno, [b.name for b in writes]))
        self._wait(e, self._deps(reads, writes))
        inst = fn()
        inst.then_inc(self.semobj[e], 1)
        self.cnt[e] += 1
        ev = (e, self.cnt[e])
        for b in writes:
            b.writes = {ev[0]: ev[1]}
            b.reads = {}
        for b in reads:
            if b not in writes:
                b.reads[ev[0]] = ev[1]
        self.nops += 1
        return inst

    def dma_in(self, q, dst_buf, dst_ap, src_ap, first=True, **kw):
        if self.nops >= self.limit:
            return
        if first:
            dst_buf.war = self._deps((), (dst_buf,))
        self._wait(q, dst_buf.war)
        s = dst_buf.dsem()
        self.eng[q].dma_start(out=dst_ap, in_=src_ap, **kw).then_inc(self.semobj[s], 16)
        dst_buf.dcount += 16
        dst_buf.writes = {s: dst_buf.dcount}
        dst_buf.reads = {}

    def dma_out(self, q, src_buf, dst_ap, src_ap, **kw):
        if self.nops >= self.limit:
            return
        self._wait(q, self._deps((src_buf,), ()))
        s = src_buf.rsem()
        self.eng[q].dma_start(out=dst_ap, in_=src_ap, **kw).then_inc(self.semobj[s], 16)
        src_buf.rcount += 16
        src_buf.reads[s] = src_buf.rcount
        self.dma_sems[s] = src_buf.rcount

    def finish(self):
        self.barrier()


import numpy as np

D = 2048
DC = D // 128
DFF = 5632
FC = DFF // 128
TT = 512
EPS = 1e-6


class WStream:
    def __init__(self, c, nbuf=8):
        self.c = c
        self.bufs = [c.sbuf(f"wst{i}", [128, 2048], BF16) for i in range(nbuf)]
        self.items = []
        self.issued = 0
        self.taken = 0

    def add(self, ap):
        self.items.append(ap)
        return len(self.items) - 1

    def _issue_upto(self, n):
        n = min(n, len(self.items))
        while self.issued < n:
            i = self.issued
            b = self.bufs[i % len(self.bufs)]
            ap = self.items[i]
            ne = ap.shape[-1]
            self.c.dma_in("pool", b, b[:, 0:ne], ap)
            self.issued += 1

    def take(self):
        i = self.taken
        self._issue_upto(i + len(self.bufs) - 2)
        if self.issued <= i:
            self._issue_upto(i + 1)
        self.taken += 1
        return self.bufs[i % len(self.bufs)]


def tile_w(W):
    K, N = W.shape
    Np = -(-N // 128) * 128
    if Np != N:
        W = np.concatenate([W, np.zeros((K, Np - N), W.dtype)], axis=1)
    return np.ascontiguousarray(
        W.reshape(K // 128, 128, Np // 128, 128).transpose(2, 1, 0, 3).reshape(Np // 128, 128, K))


def gains_layout(gs):
    return np.ascontiguousarray(
        np.stack([g.reshape(DC, 128).T for g in gs], axis=1).reshape(128, len(gs) * DC)).astype(np.float32)


def emit_dense(nc, c, io, NT, has_mix, has_ffn, n_proj_ct, write_x, name="dense"):
    ng = (1 if has_mix else 0) + (2 if has_ffn else 0) + (1 if n_proj_ct else 0)
    xT = io["xT"]; gains = io["gains"]
    if has_mix:
        yT = io["yT"]; w_o = io["w_o"]
    if has_ffn:
        w_g = io["w_g"]; w_u = io["w_u"]; w_d = io["w_d"]
    if n_proj_ct:
        w_p = io["w_p"]; po = io["po"]
    if write_x:
        xo = io["xo"]
    X = c.sbuf("X", [128, DC, TT], F32)
    MT = c.sbuf("MT", [128, DC, TT], F32)
    HT = c.sbuf("HT", [128, DC, TT], BF16)
    HID = c.sbuf("HID", [128, FC, TT], BF16)
    RSTD = c.sbuf("RSTD", [128, TT], F32)
    TMP = [c.sbuf(f"TMP{i}", [128, TT], F32) for i in range(3)]
    G = c.sbuf("G", [128, ng * DC], F32)
    ONES = c.sbuf("ONES", [128, 128], BF16)
    EPSB = c.sbuf("EPSB", [128, 1], F32)
    PS = [c.psum(f"PS{i}", [128, TT], F32) for i in range(8)]
    ws = WStream(c, nbuf=8)
    psi = [0]

    def nextps():
        p = PS[psi[0] % 8]
        psi[0] += 1
        return p

    c.dma_in("sp", G, G[:], gains[:, :])
    c.op("dve", lambda: nc.vector.memset(ONES[:], 1.0), writes=[ONES])
    c.op("dve", lambda: nc.vector.memset(EPSB[:], EPS), writes=[EPSB])

    xT_v = xT.rearrange("(c p) t -> p c t", p=128)
    if has_mix:
        yT_v = yT.rearrange("(c p) t -> p c t", p=128)
    if write_x:
        xo_v = xo.rearrange("(c p) t -> p c t", p=128)

    ntiles = NT // TT
    for j in range(ntiles):
        if has_mix:
            for dc in range(DC):
                ws.add(w_o[dc])
        if has_ffn:
            for fc in range(FC):
                ws.add(w_g[fc])
                ws.add(w_u[fc])
            for dc in range(DC):
                ws.add(w_d[dc, :, 0:2048])
                ws.add(w_d[dc, :, 2048:4096])
                ws.add(w_d[dc, :, 4096:5632])
        for ct in range(n_proj_ct):
            ws.add(w_p[ct])

    tmpi = [0]

    def nexttmp():
        t = TMP[tmpi[0] % 3]
        tmpi[0] += 1
        return t

    def rstd_from_sq():
        p = nextps()

        def mm():
            for k in range(DC):
                i = nc.tensor.matmul(p[:], ONES[:], HID[:, k, :], start=(k == 0), stop=(k == DC - 1))
            return i
        c.op("pe", mm, reads=[ONES, HID], writes=[p])
        c.op("act", lambda: nc.scalar.activation(out=RSTD[:], in_=p[:], func=AF.Sqrt, scale=1.0 / D, bias=EPSB[:]),
             reads=[p, EPSB], writes=[RSTD])
        c.op("dve", lambda: nc.vector.reciprocal(out=RSTD[:], in_=RSTD[:]), reads=[RSTD], writes=[RSTD])

    def norm_to_HT(gi):
        for k in range(DC):
            c.op("act", lambda k=k: nc.scalar.activation(out=HID[:, k, :], in_=X[:, k, :], func=AF.Square),
                 reads=[X], writes=[HID])
        rstd_from_sq()
        for k in range(DC):
            c.op("dve", lambda k=k: nc.vector.scalar_tensor_tensor(
                out=HT[:, k, :], in0=X[:, k, :], scalar=G[:, gi * DC + k:gi * DC + k + 1], in1=RSTD[:],
                op0=ALU.mult, op1=ALU.mult), reads=[X, G, RSTD], writes=[HT])

    def add_normed_MT(gi):
        rstd_from_sq()
        for k in range(DC):
            t = nexttmp()
            c.op("dve", lambda k=k, t=t: nc.vector.scalar_tensor_tensor(
                out=t[:], in0=MT[:, k, :], scalar=G[:, gi * DC + k:gi * DC + k + 1], in1=RSTD[:],
                op0=ALU.mult, op1=ALU.mult), reads=[MT, G, RSTD], writes=[t])
            c.op("dve", lambda k=k, t=t: nc.vector.tensor_tensor(out=X[:, k, :], in0=X[:, k, :], in1=t[:], op=ALU.add),
                 reads=[X, t], writes=[X])

    def proj_chunk(rhs_buf, nk, wbufs, koffs):
        p = nextps()

        def mm():
            k = 0
            i = None
            for (wb, n) in wbufs:
                for kk in range(n):
                    i = nc.tensor.matmul(p[:], wb[:, kk * 128:(kk + 1) * 128], rhs_buf[:, k, :],
                                         start=(k == 0), stop=(k == nk - 1))
                    k += 1
            return i
        c.op("pe", mm, reads=[rhs_buf] + [w for w, _ in wbufs], writes=[p])
        return p

    for j in range(ntiles):
        ts = slice(j * TT, (j + 1) * TT)
        c.dma_in("sp", X, X[:], xT_v[:, :, ts])
        gi = 0
        if has_mix:
            c.dma_in("sp", HT, HT[:], yT_v[:, :, ts])
            for dc in range(DC):
                wb = ws.take()
                p = proj_chunk(HT, DC, [(wb, DC)], None)
                c.op("act", lambda dc=dc, p=p: nc.scalar.activation(out=MT[:, dc, :], in_=p[:], func=AF.Copy),
                     reads=[p], writes=[MT])
                c.op("act", lambda dc=dc, p=p: nc.scalar.activation(out=HID[:, dc, :], in_=p[:], func=AF.Square),
                     reads=[p], writes=[HID])
            add_normed_MT(gi)
            gi += 1
        if has_ffn:
            norm_to_HT(gi)
            gi += 1
            for fc in range(FC):
                wg = ws.take()
                pg = proj_chunk(HT, DC, [(wg, DC)], None)
                wu = ws.take()
                pu = proj_chunk(HT, DC, [(wu, DC)], None)
                t = nexttmp()
                c.op("act", lambda t=t, pg=pg: nc.scalar.activation(out=t[:], in_=pg[:], func=AF.Silu),
                     reads=[pg], writes=[t])
                c.op("dve", lambda fc=fc, t=t, pu=pu: nc.vector.tensor_tensor(out=HID[:, fc, :], in0=t[:], in1=pu[:], op=ALU.mult),
                     reads=[t, pu], writes=[HID])
            for dc in range(DC):
                w0 = ws.take()
                w1 = ws.take()
                w2 = ws.take()
                p = proj_chunk(HID, FC, [(w0, 16), (w1, 16), (w2, 12)], None)
                c.op("act", lambda dc=dc, p=p: nc.scalar.activation(out=MT[:, dc, :], in_=p[:], func=AF.Copy),
                     reads=[p], writes=[MT])
            for dc in range(DC):
                c.op("act", lambda dc=dc: nc.scalar.activation(out=HID[:, dc, :], in_=MT[:, dc, :], func=AF.Square),
                     reads=[MT], writes=[HID])
            add_normed_MT(gi)
            gi += 1
        if write_x:
            c.dma_out("sp", X, xo_v[:, :, ts], X[:])
        if n_proj_ct:
            norm_to_HT(gi)
            for ct in range(n_proj_ct):
                wb = ws.take()
                p = proj_chunk(HT, DC, [(wb, DC)], None)
                t = nexttmp()
                if ct % 2 == 0:
                    c.op("act", lambda t=t, p=p: nc.scalar.activation(out=t[:], in_=p[:], func=AF.Copy),
                         reads=[p], writes=[t])
                else:
                    c.op("dve", lambda t=t, p=p: nc.vector.tensor_copy(out=t[:], in_=p[:]), reads=[p], writes=[t])
                c.dma_out("sp", t, po[ct * 128:(ct + 1) * 128, ts], t[:])
    print(f"[{name}] ops={c.nops} waits={c.nwaits}")


import numpy as np

TT = 512
NCH = 8
C = 64
NCT = 27
RW = 1024
RWKV_PROJ = 3520
LW_SCALE = 0.6065306597126334


def half_cols(hh):
    cols = []
    def seg(a, n, pad):
        cols.extend(list(range(a, a + n)) + [-1] * (pad - n))
    seg(0 + hh * 512, 512, 512)
    seg(1024 + hh * 512, 512, 512)
    seg(2048 + hh * 512, 512, 512)
    seg(3072, 96, 128)
    seg(3168, 96, 128)
    seg(3264, 256, 256)
    o = RWKV_PROJ
    seg(o + hh * 512, 512, 512)
    seg(o + 1024 + hh * 512, 512, 512)
    seg(o + 2048 + hh * 128, 128, 128)
    seg(o + 2304 + hh * 128, 128, 128)
    seg(o + 2560 + hh * 8, 8, 128)
    return np.array(cols)


def gather_cols(W, cols):
    out = np.zeros((W.shape[0], len(cols)), W.dtype)
    m = cols >= 0
    out[:, m] = W[:, cols[m]]
    return out


def fm(v, n=128):
    return np.ascontiguousarray(v.reshape(-1, n).T)


def mixer_params(hh, I):
    f32 = np.float32
    cols = half_cols(hh)
    mu = np.zeros(16 * 128, f32)
    rc = cols[:16 * 128]
    mu[rc >= 0] = I["rwkv_mu"][rc[rc >= 0]]
    hs = slice(hh * 512, (hh + 1) * 512)
    P = {}
    par = [fm(mu)]
    par.append(fm(I["rwkv_w0"][hs]))
    par.append(fm(I["rwkv_a0"][hs]))
    par.append(fm(I["rwkv_k_k"][hs]))
    par.append(fm(I["rwkv_k_a"][hs]))
    par.append(fm(I["rwkv_r_k"].reshape(-1)[hs]))
    par.append(fm(I["rwkv_ln_w"][hs]))
    par.append(fm(I["rwkv_ln_b"][hs]))
    cw = I["ssm_conv_w"]
    cb = I["ssm_conv_b"]
    cidx = np.concatenate([np.arange(hh * 512, hh * 512 + 512), 1024 + hh * 128 + np.arange(128),
                           1280 + hh * 128 + np.arange(128)])
    for j in range(4):
        par.append(fm(cw[j][cidx]))
    par.append(fm(cb[cidx]))
    par.append(fm(np.repeat(I["ssm_D"][hh * 8:(hh + 1) * 8], 64)))
    par.append(fm(I["ssm_norm_w"][hs]))
    P["par"] = np.concatenate(par, axis=1).astype(f32)
    hp = np.zeros((8, 2), f32)
    hp[:, 0] = I["ssm_dt_bias"][hh * 8:(hh + 1) * 8]
    hp[:, 1] = I["ssm_A_log"][hh * 8:(hh + 1) * 8]
    P["hpar"] = hp
    w2 = np.zeros((128, 512), f32); w2[:96] = I["rwkv_w2"][:, hs]
    a2 = np.zeros((128, 512), f32); a2[:96] = I["rwkv_a2"][:, hs]
    g2 = I["rwkv_g2"][:, hs]
    P["lora"] = np.ascontiguousarray(np.concatenate([w2, a2, g2[:128], g2[128:]], axis=1)).astype(f32)
    return P


def rwkv_consts():
    f32 = np.float32
    p = np.arange(128)[:, None]
    j = np.arange(128)[None, :]
    idf = (p == j).astype(f32)
    msu = (j > p).astype(f32)
    miu = (j >= p).astype(f32)
    msl = (j < p).astype(f32)
    ob = ((p // 64) == (j // 64)).astype(f32)
    reset = np.ones((128, 512), f32)
    reset[:, ::64] = 0.0
    return np.ascontiguousarray(np.concatenate([idf, msu, miu, msl, ob, reset], axis=1))


def emit_rwkv(nc, c, io, T):
    pT = io["pT"]; par = io["par"]; lora = io["lora"]; cst = io["cst"]; yT = io["yT"]
    V, A_, G_, P_, PE = nc.vector, nc.scalar, nc.gpsimd, nc.gpsimd, nc.tensor

    PAR = c.sbuf("PAR", [128, 82], F32)
    OMKA = c.sbuf("OMKA", [128, 4], F32)
    LORA = c.sbuf("LORA", [128, 2048], BF16)
    CST = c.sbuf("CST", [128, 1152], F32)
    IDB = c.sbuf("IDB", [128, 128], BF16)
    ONESB = c.sbuf("ONESB", [128, 128], BF16)
    EPSL = c.sbuf("EPSL", [128, 1], F32)
    PR = c.sbuf("PR", [128, 16, 513], F32)
    CAR = c.sbuf("CAR", [128, 16], F32)
    TW = c.sbuf("TW", [128, 512], BF16)
    ACb = c.sbuf("ACb", [128, 512], BF16)
    SG = c.sbuf("SG", [128, 2, 512], BF16)
    NT_ = 14
    TM = [c.sbuf(f"TM{i}", [128, 512], F32) for i in range(NT_)]
    TB = [c.sbuf(f"TBF{i}", [128, 512], BF16) for i in range(3)]
    GATE = [c.sbuf(f"GATE{i}", [128, 512], F32) for i in range(2)]
    BON = [c.sbuf(f"BON{i}", [128, 512], F32) for i in range(2)]
    GAM = [c.sbuf(f"GAM{i}", [128, 8], F32) for i in range(2)]
    ARd = [c.sbuf(f"ARd{i}", [128, 8, 2, 128], BF16) for i in range(2)]
    BKd = [c.sbuf(f"BKd{i}", [128, 8, 2, 128], BF16) for i in range(2)]
    BHd = [c.sbuf(f"BHd{i}", [128, 8, 128], BF16) for i in range(2)]
    KHd = [c.sbuf(f"KHd{i}", [128, 8, 128], BF16) for i in range(2)]
    VDd = [c.sbuf(f"VDd{i}", [128, 8, 128], BF16) for i in range(2)]
    HF = [c.sbuf(f"HF{i}", [128, 128], F32) for i in range(4)]
    HB = [c.sbuf(f"HB{i}", [128, 128], BF16) for i in range(4)]
    NR = 3
    TOK = [c.sbuf(f"TOK{i}", [128, 4, 128], BF16) for i in range(NR)]
    X0 = [c.sbuf(f"X0_{i}", [128, 2, 128], BF16) for i in range(NR)]
    XA = [c.sbuf(f"XA{i}", [128, 2, 128], BF16) for i in range(NR)]
    XB = [c.sbuf(f"XB{i}", [128, 2, 128], BF16) for i in range(NR)]
    NA = [c.sbuf(f"NA{i}", [128, 2, 128], BF16) for i in range(NR)]
    NB0 = [c.sbuf(f"NB0_{i}", [128, 128], BF16) for i in range(NR)]
    NB1 = [c.sbuf(f"NB1_{i}", [128, 128], BF16) for i in range(NR)]
    ARB = [c.sbuf(f"ARB{i}", [128, 128], BF16) for i in range(NR)]
    ARK = [c.sbuf(f"ARK{i}", [128, 128], BF16) for i in range(NR)]
    TTb = [c.sbuf(f"TTb{i}", [128, 128], BF16) for i in range(NR)]
    PW = [c.sbuf(f"PW{i}", [128, 2, 128], BF16) for i in range(NR)]
    UB = [c.sbuf(f"UB{i}", [128, 128], BF16) for i in range(NR)]
    YF = c.sbuf("YF", [128, 512], F32)
    OUTB = [c.sbuf(f"OUTB{i}", [128, 512], BF16) for i in range(2)]
    PY = c.psum("PY", [128, 8, 128], F32)
    PA = c.psum("PA", [128, 512], F32)
    PB = c.psum("PB", [128, 512], F32)
    PI = [c.psum(f"PI{i}", [128, 512], F32) for i in range(2)]
    PU = c.psum("PU", [128, 512], F32)
    PT = c.psum("PT", [128, 4, 128], BF16)

    c.dma_in("sp", PAR, PAR[:], par[:, :])
    c.dma_in("sp", CST, CST[:], cst[:, :])
    c.dma_in("pool", LORA, LORA[:], lora[:, :])
    IDF = CST[:, 0:128]; MSU = CST[:, 128:256]; MIU = CST[:, 256:384]; MSL = CST[:, 384:512]
    RESET = CST[:, 640:1152]
    c.op("dve", lambda: V.tensor_copy(out=IDB[:], in_=CST[:, 0:128]), reads=[CST], writes=[IDB])
    c.op("dve", lambda: V.tensor_copy(out=ONESB[:], in_=CST[:, 512:640]), reads=[CST], writes=[ONESB])
    c.op("dve", lambda: V.memset(EPSL[:], 64e-5), writes=[EPSL])
    c.op("dve", lambda: V.tensor_scalar(out=OMKA[:], in0=PAR[:, 28:32], scalar1=-1.0, scalar2=1.0, op0=ALU.mult, op1=ALU.add),
         reads=[PAR], writes=[OMKA])
    c.op("pool", lambda: G_.memset(PR[:], 0.0), writes=[PR])
    for b in ARd + BKd + BHd + KHd + VDd:
        c.op("pool", lambda b=b: G_.memset(b[:], 0.0), writes=[b])
    for i in range(4):
        c.op("pool", lambda i=i: G_.memset(HF[i][:], 0.0), writes=[HF[i]])
        c.op("pool", lambda i=i: G_.memset(HB[i][:], 0.0), writes=[HB[i]])
    for i in range(NR):
        c.op("dve", lambda i=i: V.tensor_copy(out=X0[i][:, 1, :], in_=CST[:, 0:128]), reads=[CST], writes=[X0[i]])

    pT_v = pT.rearrange("(c p) t -> p c t", p=128)
    tmi = [0]

    def tm():
        t = TM[tmi[0] % NT_]
        tmi[0] += 1
        return t

    def halves(fn_engine, eng, out_buf, out_sel, make):
        for hb in range(2):
            ps = slice(hb * 64, hb * 64 + 64)
            cs = slice(hb * 64, hb * 64 + 64)
            make(ps, cs)

    unit = [0]
    ntile = T // TT
    for j in range(ntile):
        ts = slice(j * TT, (j + 1) * TT)
        c.dma_in("sp", PR, PR[:, :, 1:513], pT_v[:, :, ts])
        c.op("pool", lambda: G_.tensor_copy(out=CAR[:], in_=PR[:, :, 512]), reads=[PR], writes=[CAR])
        for ct in range(16):
            d = tm()
            c.op("pool", lambda ct=ct, d=d: G_.tensor_tensor(out=d[:], in0=PR[:, ct, 0:512], in1=PR[:, ct, 1:513], op=ALU.subtract),
                 reads=[PR], writes=[d])
            c.op("dve", lambda ct=ct, d=d: V.scalar_tensor_tensor(out=PR[:, ct, 1:513], in0=d[:], scalar=PAR[:, ct:ct + 1],
                                                                    in1=PR[:, ct, 1:513], op0=ALU.mult, op1=ALU.add),
                 reads=[PR, d, PAR], writes=[PR])
        c.op("act", lambda: A_.activation(out=TW[:], in_=PR[:, 12, 1:513], func=AF.Tanh), reads=[PR], writes=[TW])
        c.op("act", lambda: A_.activation(out=ACb[:], in_=PR[:, 13, 1:513], func=AF.Copy), reads=[PR], writes=[ACb])
        c.op("act", lambda: A_.activation(out=SG[:], in_=PR[:, 14:16, 1:513], func=AF.Sigmoid), reads=[PR], writes=[SG])
        for q in range(4):
            s_ = q % 2
            r_ = PR[:, q, 1:513]; k_ = PR[:, 4 + q, 1:513]; v_ = PR[:, 8 + q, 1:513]
            c.op("pe", lambda: PE.matmul(PA[:], LORA[:, q * 128:(q + 1) * 128], TW[:], start=True, stop=True),
                 reads=[LORA, TW], writes=[PA])
            SIG = tm()
            c.op("act", lambda: A_.activation(out=SIG[:], in_=PA[:], func=AF.Sigmoid, bias=PAR[:, 16 + q:17 + q]),
                 reads=[PA, PAR], writes=[SIG])
            c.op("pe", lambda: PE.matmul(PB[:], LORA[:, 512 + q * 128:512 + (q + 1) * 128], ACb[:], start=True, stop=True),
                 reads=[LORA, ACb], writes=[PB])
            AA = tm()
            c.op("act", lambda: A_.activation(out=AA[:], in_=PB[:], func=AF.Sigmoid, bias=PAR[:, 20 + q:21 + q]),
                 reads=[PB, PAR], writes=[AA])

            def mmg():
                PE.matmul(PU[:], LORA[:, 1024 + q * 128:1024 + (q + 1) * 128], SG[:, 0, :], start=True, stop=False)
                return PE.matmul(PU[:], LORA[:, 1536 + q * 128:1536 + (q + 1) * 128], SG[:, 1, :], start=False, stop=True)
            c.op("pe", mmg, reads=[LORA, SG], writes=[PU])
            c.op("act", lambda: A_.activation(out=GATE[s_][:], in_=PU[:], func=AF.Copy), reads=[PU], writes=[GATE[s_]])
            CS = tm()
            c.op("dve", lambda: V.tensor_tensor_scan(out=CS[:], data0=RESET, data1=SIG[:], initial=0.0, op0=ALU.mult, op1=ALU.add),
                 reads=[CST, SIG], writes=[CS])
            KK = tm()
            c.op("pool", lambda: G_.tensor_scalar(out=KK[:], in0=k_, scalar1=PAR[:, 24 + q:25 + q], scalar2=None, op0=ALU.mult),
                 reads=[PR, PAR], writes=[KK])
            SQ = TB[0]
            c.op("act", lambda: A_.activation(out=SQ[:], in_=KK[:], func=AF.Square), reads=[KK], writes=[SQ])
            c.op("pe", lambda: PE.matmul(PA[:], ONESB[:], SQ[:], start=True, stop=True), reads=[ONESB, SQ], writes=[PA])
            RN = tm()
            c.op("act", lambda: A_.activation(out=RN[:], in_=PA[:], func=AF.Sqrt), reads=[PA], writes=[RN])
            c.op("dve", lambda: V.tensor_scalar(out=RN[:], in0=RN[:], scalar1=1e-12, scalar2=None, op0=ALU.max), reads=[RN], writes=[RN])
            c.op("dve", lambda: V.reciprocal(out=RN[:], in_=RN[:]), reads=[RN], writes=[RN])
            KKN = tm()
            c.op("dve", lambda: V.tensor_tensor(out=KKN[:], in0=KK[:], in1=RN[:], op=ALU.mult), reads=[KK, RN], writes=[KKN])
            T1 = tm()
            c.op("dve", lambda: V.tensor_scalar(out=T1[:], in0=AA[:], scalar1=PAR[:, 28 + q:29 + q], scalar2=OMKA[:, q:q + 1],
                                               op0=ALU.mult, op1=ALU.add), reads=[AA, PAR, OMKA], writes=[T1])
            KM = tm()
            c.op("pool", lambda: G_.tensor_tensor(out=KM[:], in0=k_, in1=T1[:], op=ALU.mult), reads=[PR, T1], writes=[KM])
            BV = tm()
            c.op("pool", lambda: G_.tensor_tensor(out=BV[:], in0=KKN[:], in1=AA[:], op=ALU.mult), reads=[KKN, AA], writes=[BV])
            E1 = tm(); E2 = tm(); E3 = tm(); E4 = tm()
            c.op("act", lambda: A_.activation(out=E1[:], in_=CS[:], func=AF.Exp, scale=-LW_SCALE), reads=[CS], writes=[E1])
            c.op("act", lambda: A_.activation(out=E2[:], in_=CS[:], func=AF.Exp, scale=LW_SCALE), reads=[CS], writes=[E2])
            c.op("pool", lambda: G_.tensor_tensor(out=E3[:], in0=CS[:], in1=SIG[:], op=ALU.subtract), reads=[CS, SIG], writes=[E3])
            c.op("act", lambda: A_.activation(out=E3[:], in_=E3[:], func=AF.Exp, scale=-LW_SCALE), reads=[E3], writes=[E3])
            CS3 = CS[:].rearrange("p (c t) -> p c t", t=64)
            c.op("pool", lambda: G_.tensor_tensor(out=E4[:].rearrange("p (c t) -> p c t", t=64),
                                                  in0=CS3[:, :, 63:64].to_broadcast([128, 8, 64]), in1=CS3, op=ALU.subtract),
                 reads=[CS], writes=[E4])
            c.op("act", lambda: A_.activation(out=E4[:], in_=E4[:], func=AF.Exp, scale=-LW_SCALE), reads=[E4], writes=[E4])
            c.op("act", lambda: A_.activation(out=GAM[s_][:], in_=CS3[:, :, 63], func=AF.Exp, scale=-LW_SCALE),
                 reads=[CS], writes=[GAM[s_]])
            def bd(eng, outbuf, sel, in0, in1, neg=False):
                for hb in range(2):
                    ps = slice(hb * 64, hb * 64 + 64)
                    if sel is None:
                        o = outbuf[ps, :, hb * 64:hb * 64 + 64]
                    else:
                        o = outbuf[ps, :, sel, hb * 64:hb * 64 + 64]
                    a0 = in0[ps].rearrange("p (c t) -> p c t", t=64)
                    if in1 is None:
                        c.op(eng, lambda: (G_ if eng == "pool" else V).tensor_copy(out=o, in_=a0), reads=[PR], writes=[outbuf])
                        continue
                    a1 = in1[ps].rearrange("p (c t) -> p c t", t=64)
                    if neg:
                        c.op("dve", lambda: V.scalar_tensor_tensor(out=o, in0=a0, scalar=-1.0, in1=a1, op0=ALU.mult, op1=ALU.mult),
                             reads=[PR] + bd_reads, writes=[outbuf])
                    elif eng == "pool":
                        c.op("pool", lambda: G_.tensor_tensor(out=o, in0=a0, in1=a1, op=ALU.mult), reads=[PR] + bd_reads, writes=[outbuf])
                    else:
                        c.op("dve", lambda: V.tensor_tensor(out=o, in0=a0, in1=a1, op=ALU.mult), reads=[PR] + bd_reads, writes=[outbuf])
            bd_reads = [KKN, E3]; bd("dve", ARd[s_], 0, KKN[:], E3[:], neg=True)
            bd_reads = [E1]; bd("pool", ARd[s_], 1, r_, E1[:])
            bd_reads = [BV, E2]; bd("dve", BKd[s_], 0, BV[:], E2[:])
            bd_reads = [KM, E2]; bd("pool", BKd[s_], 1, KM[:], E2[:])
            bd_reads = [BV, E4]; bd("dve", BHd[s_], None, BV[:], E4[:])
            bd_reads = [KM, E4]; bd("pool", KHd[s_], None, KM[:], E4[:])
            bd_reads = []; bd("pool", VDd[s_], None, v_, None)
            T2 = tm()
            c.op("pool", lambda: G_.tensor_tensor(out=T2[:], in0=r_, in1=KM[:], op=ALU.mult), reads=[PR, KM], writes=[T2])
            T2b = TB[1]
            c.op("dve", lambda: V.tensor_scalar(out=T2b[:], in0=T2[:], scalar1=PAR[:, 32 + q:33 + q], scalar2=None, op0=ALU.mult),
                 reads=[T2, PAR], writes=[T2b])
            c.op("pe", lambda: PE.matmul(PB[:], ONESB[:], T2b[:], start=True, stop=True), reads=[ONESB, T2b], writes=[PB])
            c.op("dve", lambda: V.tensor_tensor(out=BON[s_][:], in0=PB[:], in1=v_, op=ALU.mult), reads=[PB, PR], writes=[BON[s_]])

            for ch in range(NCH):
                u = unit[0] % NR
                unit[0] += 1
                pi = PI[unit[0] % 2]
                Ad = ARd[s_][:, ch, 0, :]; Rd = ARd[s_][:, ch, 1, :]
                Bd = BKd[s_][:, ch, 0, :]; Kd = BKd[s_][:, ch, 1, :]
                AR2 = ARd[s_][:, ch, :, :].rearrange("p a b -> p (a b)")
                BK2 = BKd[s_][:, ch, :, :].rearrange("p a b -> p (a b)")

                def tr():
                    PE.transpose(PT[:, 0, :], Ad, IDB[:])
                    PE.transpose(PT[:, 1, :], BHd[s_][:, ch, :], IDB[:])
                    PE.transpose(PT[:, 2, :], KHd[s_][:, ch, :], IDB[:])
                    return PE.transpose(PT[:, 3, :], VDd[s_][:, ch, :], IDB[:])
                c.op("pe", tr, reads=[ARd[s_], BHd[s_], KHd[s_], VDd[s_], IDB], writes=[PT])
                c.op("act", lambda: A_.activation(out=TOK[u][:], in_=PT[:], func=AF.Copy), reads=[PT], writes=[TOK[u]])

                def g1():
                    PE.matmul(PA[:, 0:256], Bd, AR2, start=True, stop=True)
                    PE.matmul(PA[:, 256:512], Ad, BK2, start=True, stop=True)
                    return PE.matmul(PB[:, 0:128], Kd, Rd, start=True, stop=True)
                c.op("pe", g1, reads=[ARd[s_], BKd[s_]], writes=[PA, PB])
                c.op("dve", lambda: V.tensor_tensor(out=X0[u][:, 0, :], in0=PA[:, 0:128], in1=MSU, op=ALU.mult),
                     reads=[PA, CST], writes=[X0[u]])
                c.op("dve", lambda: V.tensor_tensor(out=ARB[u][:], in0=PA[:, 128:256], in1=MIU, op=ALU.mult),
                     reads=[PA, CST], writes=[ARB[u]])
                c.op("dve", lambda: V.tensor_tensor(out=NA[u][:], in0=PA[:, 256:512].rearrange("p (a b) -> p a b", a=2),
                                                    in1=MSL.unsqueeze(1).to_broadcast([128, 2, 128]), op=ALU.mult),
                     reads=[PA, CST], writes=[NA[u]])
                c.op("dve", lambda: V.tensor_tensor(out=ARK[u][:], in0=PB[:, 0:128], in1=MIU, op=ALU.mult),
                     reads=[PB, CST], writes=[ARK[u]])
                MScur = X0[u]
                Ncur_buf = NA[u]
                Ncur = NA[u][:, 0, :]
                for lev in range(6):
                    last = lev == 5
                    MSn = XA[u] if lev % 2 == 0 else XB[u]
                    Nn = NB0[u] if lev % 2 == 0 else NB1[u]

                    def lv():
                        if not last:
                            PE.matmul(pi[:, 0:256], Ncur, MScur[:].rearrange("p a b -> p (a b)"), start=True, stop=False)
                        else:
                            PE.matmul(pi[:, 128:256], Ncur, MScur[:, 1, :], start=True, stop=False)
                        i = PE.matmul(pi[:, 128:256], IDB[:], MScur[:, 1, :], start=False, stop=True)
                        if not last:
                            i = PE.matmul(pi[:, 256:384], MScur[:, 0, :], Ncur, start=True, stop=True)
                        return i
                    c.op("pe", lv, reads=[Ncur_buf, MScur, IDB], writes=[pi])
                    if not last:
                        c.op("act", lambda: A_.activation(out=MSn[:].rearrange("p a b -> p (a b)"), in_=pi[:, 0:256], func=AF.Copy),
                             reads=[pi], writes=[MSn])
                        import os
                        nv = os.environ.get("NVAR", "act")
                        if nv == "dve":
                            c.op("dve", lambda: V.tensor_scalar(out=Nn[:], in0=pi[:, 256:384], scalar1=1.0, scalar2=None, op0=ALU.mult), reads=[pi], writes=[Nn])
                        elif nv == "dvemask":
                            c.op("dve", lambda: V.tensor_tensor(out=Nn[:], in0=pi[:, 256:384], in1=MSL, op=ALU.mult), reads=[pi, CST], writes=[Nn])
                        else:
                            c.op("act", lambda: A_.activation(out=Nn[:], in_=pi[:, 256:384], func=AF.Copy), reads=[pi], writes=[Nn])
                        MScur = MSn
                        Ncur_buf = Nn
                        Ncur = Nn[:]
                    else:
                        c.op("act", lambda: A_.activation(out=TTb[u][:], in_=pi[:, 128:256], func=AF.Copy), reads=[pi], writes=[TTb[u]])

                def pw():
                    PE.matmul(PB[:, 128:256], TOK[u][:, 0, :], TTb[u][:], start=True, stop=True)
                    return PE.matmul(PB[:, 256:384], NA[u][:, 1, :], TTb[u][:], start=True, stop=True)
                c.op("pe", pw, reads=[TOK[u], NA[u], TTb[u]], writes=[PB])
                c.op("act", lambda: A_.activation(out=PW[u][:].rearrange("p a b -> p (a b)"), in_=PB[:, 128:384], func=AF.Copy),
                     reads=[PB], writes=[PW[u]])

                def um():
                    PE.matmul(PU[:, 0:128], PW[u][:, 1, :], TOK[u][:, 3, :], start=True, stop=False)
                    return PE.matmul(PU[:, 0:128], PW[u][:, 0, :], HB[q][:], start=False, stop=True)
                c.op("pe", um, reads=[PW[u], TOK[u], HB[q]], writes=[PU])
                c.op("act", lambda: A_.activation(out=UB[u][:], in_=PU[:, 0:128], func=AF.Copy), reads=[PU], writes=[UB[u]])

                def hy():
                    PE.matmul(PU[:, 128:256], TOK[u][:, 2, :], TOK[u][:, 3, :], start=True, stop=False)
                    PE.matmul(PU[:, 128:256], TOK[u][:, 1, :], UB[u][:], start=False, stop=True)
                    PE.matmul(PY[:, ch, :], HB[q][:], Rd, start=True, stop=False)
                    PE.matmul(PY[:, ch, :], TOK[u][:, 3, :], ARK[u][:], start=False, stop=False)
                    return PE.matmul(PY[:, ch, :], UB[u][:], ARB[u][:], start=False, stop=True)
                c.op("pe", hy, reads=[TOK[u], UB[u], HB[q], ARd[s_], ARK[u], ARB[u]], writes=[PU, PY])
                c.op("dve", lambda: V.scalar_tensor_tensor(out=HF[q][:], in0=HF[q][:], scalar=GAM[s_][:, ch:ch + 1], in1=PU[:, 128:256],
                                                           op0=ALU.mult, op1=ALU.add), reads=[HF[q], GAM[s_], PU], writes=[HF[q]])
                c.op("act", lambda: A_.activation(out=HB[q][:], in_=HF[q][:], func=AF.Copy), reads=[HF[q]], writes=[HB[q]])

            c.op("act", lambda: A_.activation(out=YF[0:64, :].rearrange("p (c t) -> p c t", t=64), in_=PY[0:64, :, 0:64], func=AF.Copy),
                 reads=[PY], writes=[YF])
            c.op("dve", lambda: V.tensor_scalar(out=YF[64:128, :].rearrange("p (c t) -> p c t", t=64), in0=PY[64:128, :, 64:128], scalar1=1.0, scalar2=None, op0=ALU.mult),
                 reads=[PY], writes=[YF])
            YB = TB[2]; YS = TB[0]
            c.op("act", lambda: A_.activation(out=YB[:], in_=YF[:], func=AF.Copy), reads=[YF], writes=[YB])
            c.op("act", lambda: A_.activation(out=YS[:], in_=YF[:], func=AF.Square), reads=[YF], writes=[YS])
            c.op("pe", lambda: PE.matmul(PA[:], ONESB[:], YB[:], start=True, stop=True), reads=[ONESB, YB], writes=[PA])
            c.op("pe", lambda: PE.matmul(PB[:], ONESB[:], YS[:], start=True, stop=True), reads=[ONESB, YS], writes=[PB])
            MEAN = tm(); M2 = tm(); VAR = tm()
            c.op("act", lambda: A_.activation(out=MEAN[:], in_=PA[:], func=AF.Copy, scale=1.0 / 64), reads=[PA], writes=[MEAN])
            c.op("pool", lambda: G_.tensor_tensor(out=M2[:], in0=MEAN[:], in1=MEAN[:], op=ALU.mult), reads=[MEAN], writes=[M2])
            c.op("dve", lambda: V.scalar_tensor_tensor(out=VAR[:], in0=PB[:], scalar=1.0 / 64, in1=M2[:], op0=ALU.mult, op1=ALU.subtract),
                 reads=[PB, M2], writes=[VAR])
            c.op("act", lambda: A_.activation(out=VAR[:], in_=VAR[:], func=AF.Sqrt, bias=EPSL[:]), reads=[VAR, EPSL], writes=[VAR])
            c.op("dve", lambda: V.reciprocal(out=VAR[:], in_=VAR[:]), reads=[VAR], writes=[VAR])
            c.op("pool", lambda: G_.tensor_tensor(out=YF[:], in0=YF[:], in1=MEAN[:], op=ALU.subtract), reads=[YF, MEAN], writes=[YF])
            c.op("pool", lambda: G_.tensor_tensor(out=YF[:], in0=YF[:], in1=VAR[:], op=ALU.mult), reads=[YF, VAR], writes=[YF])
            c.op("dve", lambda: V.tensor_scalar(out=YF[:], in0=YF[:], scalar1=PAR[:, 36 + q:37 + q], scalar2=PAR[:, 40 + q:41 + q],
                                               op0=ALU.mult, op1=ALU.add), reads=[YF, PAR], writes=[YF])
            c.op("pool", lambda: G_.tensor_tensor(out=YF[:], in0=YF[:], in1=BON[s_][:], op=ALU.add), reads=[YF, BON[s_]], writes=[YF])
            ob = OUTB[q % 2]
            c.op("dve", lambda: V.tensor_tensor(out=ob[:], in0=YF[:], in1=GATE[s_][:], op=ALU.mult), reads=[YF, GATE[s_]], writes=[ob])
            c.dma_out("sp", ob, yT[q * 128:(q + 1) * 128, ts], ob[:])
        c.op("pool", lambda: G_.tensor_copy(out=PR[:, :, 0], in_=CAR[:]), reads=[CAR], writes=[PR])
    print(f"[rwkv] ops={c.nops} waits={c.nwaits}")


def ssd_consts():
    f32 = np.float32
    p = np.arange(128)[:, None]
    j = np.arange(128)[None, :]
    idf = (p == j).astype(f32)
    valid = ((p // 64) == (j // 64)) & (j >= p)
    neg2 = np.where(valid, 0.0, -1e5).astype(f32)
    ones = np.ones((128, 128), f32)
    reset = np.ones((128, 512), f32)
    reset[:, ::64] = 0.0
    cst = np.concatenate([idf, neg2, ones, reset], axis=1)
    selp = np.zeros((8, 4, 128), f32)
    for h in range(8):
        selp[h, h // 2, (h % 2) * 64:(h % 2) * 64 + 64] = 1.0
    selh = np.zeros((8, 8, 128), f32)
    for h in range(8):
        selh[h, h, :] = 1.0
    sel = np.concatenate([selp.reshape(8, 512), selh.reshape(8, 1024)], axis=1)
    return np.ascontiguousarray(cst), np.ascontiguousarray(sel)


def emit_ssd(nc, c, io, T):
    pT = io["pT"]; par = io["par"]; hpar = io["hpar"]; cst = io["cst"]; sel = io["sel"]; yT = io["yT"]
    V, A_, G_, PE = nc.vector, nc.scalar, nc.gpsimd, nc.tensor

    PAR = c.sbuf("PAR", [128, 82], F32)
    HPAR = c.sbuf("HPAR", [8, 2], F32)
    ANEG = c.sbuf("ANEG", [8, 1], F32)
    CST = c.sbuf("CST", [128, 896], F32)
    SEL = c.sbuf("SEL", [8, 1536], F32)
    IDB = c.sbuf("IDB", [128, 128], BF16)
    ONESF = c.sbuf("ONESF", [128, 128], BF16)
    EPSB = c.sbuf("EPSB", [128, 1], F32)
    Z = c.sbuf("Z", [128, 4, 512], F32)
    XSP = c.sbuf("XSP", [128, 6, 515], F32)
    DTR = c.sbuf("DTR", [8, 512], F32)
    DT = c.sbuf("DT", [8, 512], F32)
    DA = c.sbuf("DA", [8, 512], F32)
    ACS = c.sbuf("ACS", [8, 512], F32)
    XSA = c.sbuf("XSA", [128, 4, 512], F32)
    XC = c.sbuf("XC", [128, 4, 512], F32)
    CA = c.sbuf("CA", [128, 512], F32)
    BAb = c.sbuf("BAb", [128, 512], BF16)
    CAb = c.sbuf("CAb", [128, 512], BF16)
    ACSB = c.sbuf("ACSB", [128, 8, 512], F32)
    CDS = c.sbuf("CDS", [128, 8, 8], F32)
    CE = c.sbuf("CE", [128, 8, 512], BF16)
    XCT = c.sbuf("XCT", [128, 4, 512], BF16)
    XDT = c.sbuf("XDT", [128, 4, 512], BF16)
    BTK = c.sbuf("BTK", [128, 4, 128], BF16)
    ACST = c.sbuf("ACST", [128, 4, 8], F32)
    DEC = c.sbuf("DEC", [128, 4, 8], F32)
    SEG = [c.sbuf(f"SEG{i}", [128, 8, 128], F32) for i in range(2)]
    MTL = [c.sbuf(f"MTL{i}", [128, 8, 128], BF16) for i in range(2)]
    SBST = c.sbuf("SBST", [128, 8, 512], BF16)
    SF = c.sbuf("SF", [128, 512], F32)
    Y2 = c.sbuf("Y2", [128, 4, 512], F32)
    YSQ = c.sbuf("YSQ", [128, 4, 512], BF16)
    RST = c.sbuf("RST", [128, 512], F32)
    TM = [c.sbuf(f"TM{i}", [128, 512], F32) for i in range(4)]
    OUTB = [c.sbuf(f"OUTB{i}", [128, 512], BF16) for i in range(2)]
    PS = [c.psum(f"PS{i}", [128, 512], F32) for i in range(3)]
    PYQ = [c.psum(f"PYQ{i}", [128, 512], F32) for i in range(4)]
    PTB = c.psum("PTB", [128, 4, 128], BF16)
    psi = [0]

    def nps():
        p = PS[psi[0] % 3]
        psi[0] += 1
        return p
    tmi = [0]

    def tm():
        t = TM[tmi[0] % 4]
        tmi[0] += 1
        return t

    c.dma_in("sp", PAR, PAR[:], par[:, :])
    c.dma_in("sp", HPAR, HPAR[:], hpar[:, :])
    c.dma_in("sp", CST, CST[:], cst[:, :])
    c.dma_in("sp", SEL, SEL[:], sel[:, :])
    IDF = CST[:, 0:128]; NEG2 = CST[:, 128:256]; RESET = CST[:, 384:896]
    c.op("dve", lambda: V.tensor_copy(out=IDB[:], in_=CST[:, 0:128]), reads=[CST], writes=[IDB])
    c.op("dve", lambda: V.tensor_copy(out=ONESF[:], in_=CST[:, 256:384]), reads=[CST], writes=[ONESF])
    c.op("dve", lambda: V.memset(EPSB[:], 1e-6), writes=[EPSB])
    c.op("act", lambda: A_.activation(out=ANEG[:], in_=HPAR[:, 1:2], func=AF.Exp), reads=[HPAR], writes=[ANEG])
    c.op("dve", lambda: V.tensor_scalar(out=ANEG[:], in0=ANEG[:], scalar1=-1.0, scalar2=None, op0=ALU.mult), reads=[ANEG], writes=[ANEG])
    c.op("pool", lambda: G_.memset(XSP[:], 0.0), writes=[XSP])
    c.op("pool", lambda: G_.memset(SF[:], 0.0), writes=[SF])

    pT_v = pT.rearrange("(c p) t -> p c t", p=128)
    CW = lambda j, i: PAR[:, 44 + 6 * j + i:45 + 6 * j + i]
    ntile = T // TT
    for jt in range(ntile):
        ts = slice(jt * TT, (jt + 1) * TT)
        c.dma_in("sp", Z, Z[:], pT_v[:, 0:4, ts])
        c.dma_in("sp", XSP, XSP[:, :, 3:515], pT_v[:, 4:10, ts])
        c.dma_in("sp", DTR, DTR[:], pT[10 * 128:10 * 128 + 8, ts])
        for i in range(6):
            acc = tm()
            c.op("pool", lambda: G_.tensor_scalar(out=acc[:], in0=XSP[:, i, 0:512], scalar1=CW(0, i), scalar2=None, op0=ALU.mult),
                 reads=[XSP, PAR], writes=[acc])
            for j in range(1, 4):
                c.op("dve", lambda: V.scalar_tensor_tensor(out=acc[:], in0=XSP[:, i, j:j + 512], scalar=CW(j, i), in1=acc[:],
                                                           op0=ALU.mult, op1=ALU.add), reads=[XSP, PAR, acc], writes=[acc])
            if i < 4:
                c.op("act", lambda: A_.activation(out=XSA[:, i, :], in_=acc[:], func=AF.Silu, bias=PAR[:, 68 + i:69 + i]),
                     reads=[acc, PAR], writes=[XSA])
            elif i == 4:
                c.op("act", lambda: A_.activation(out=BAb[:], in_=acc[:], func=AF.Silu, bias=PAR[:, 68 + i:69 + i]),
                     reads=[acc, PAR], writes=[BAb])
            else:
                c.op("act", lambda: A_.activation(out=CA[:], in_=acc[:], func=AF.Silu, bias=PAR[:, 68 + i:69 + i]),
                     reads=[acc, PAR], writes=[CA])
                c.op("act", lambda: A_.activation(out=CAb[:], in_=CA[:], func=AF.Copy), reads=[CA], writes=[CAb])
        c.op("pool", lambda: G_.tensor_copy(out=XSP[:, :, 0:3], in_=XSP[:, :, 512:515]), reads=[XSP], writes=[XSP])
        c.op("act", lambda: A_.activation(out=DT[:], in_=DTR[:], func=AF.Exp, bias=HPAR[:, 0:1]), reads=[DTR, HPAR], writes=[DT])
        c.op("dve", lambda: V.tensor_scalar(out=DT[:], in0=DT[:], scalar1=1.0, scalar2=None, op0=ALU.add), reads=[DT], writes=[DT])
        c.op("act", lambda: A_.activation(out=DT[:], in_=DT[:], func=AF.Ln), reads=[DT], writes=[DT])
        c.op("dve", lambda: V.tensor_scalar(out=DA[:], in0=DT[:], scalar1=ANEG[:, 0:1], scalar2=None, op0=ALU.mult),
             reads=[DT, ANEG], writes=[DA])
        c.op("dve", lambda: V.tensor_tensor_scan(out=ACS[:], data0=RESET[0:8, :], data1=DA[:], initial=0.0, op0=ALU.mult, op1=ALU.add),
             reads=[CST, DA], writes=[ACS])
        for h in range(8):
            p = nps()
            c.op("pe", lambda: PE.matmul(p[:], SEL[:, 512 + h * 128:512 + (h + 1) * 128], ACS[:], start=True, stop=True),
                 reads=[SEL, ACS], writes=[p])
            c.op("act", lambda: A_.activation(out=ACSB[:, h, :], in_=p[:], func=AF.Copy), reads=[p], writes=[ACSB])
        c.op("act", lambda: A_.activation(out=CDS[:], in_=ACSB[:].rearrange("p h (c t) -> p h c t", t=64)[:, :, :, 63], func=AF.Exp),
             reads=[ACSB], writes=[CDS])
        for q in range(4):
            p = nps()
            c.op("pe", lambda: PE.matmul(p[:], SEL[:, q * 128:(q + 1) * 128], DT[:], start=True, stop=True),
                 reads=[SEL, DT], writes=[p])
            c.op("dve", lambda: V.tensor_tensor(out=XC[:, q, :], in0=XSA[:, q, :], in1=p[:], op=ALU.mult), reads=[XSA, p], writes=[XC])
        for h in range(8):
            t = tm()
            c.op("act", lambda: A_.activation(out=t[:], in_=ACSB[:, h, :], func=AF.Exp), reads=[ACSB], writes=[t])
            c.op("pool", lambda: G_.tensor_tensor(out=CE[:, h, :], in0=CA[:], in1=t[:], op=ALU.mult), reads=[CA, t], writes=[CE])
        for blk in range(4):
            bs = slice(blk * 128, (blk + 1) * 128)
            p = nps()
            c.op("pe", lambda: PE.transpose(p[:, 0:8], ACS[0:8, bs], CST[0:8, 0:8]), reads=[ACS, CST], writes=[p])
            c.op("act", lambda: A_.activation(out=ACST[:, blk, :], in_=p[:, 0:8], func=AF.Copy), reads=[p], writes=[ACST])
            for a in range(2):
                ps_ = slice(a * 64, a * 64 + 64)
                ce = blk * 128 + a * 64 + 63
                c.op("dve", lambda: V.tensor_tensor(out=DEC[ps_, blk, :], in0=ACSB[ps_, :, ce], in1=ACST[ps_, blk, :], op=ALU.subtract),
                     reads=[ACSB, ACST], writes=[DEC])
            c.op("act", lambda: A_.activation(out=DEC[:, blk, :], in_=DEC[:, blk, :], func=AF.Exp), reads=[DEC], writes=[DEC])
            c.op("pe", lambda: PE.transpose(PTB[:, 0, :], BAb[:, bs], IDB[:]), reads=[BAb, IDB], writes=[PTB])
            import os
            bv = os.environ.get("BVAR", "act")
            if bv == "act":
                c.op("act", lambda: A_.activation(out=BTK[:, blk, :], in_=PTB[:, 0, :], func=AF.Copy), reads=[PTB], writes=[BTK])
            elif bv == "dve":
                c.op("dve", lambda: V.tensor_scalar(out=BTK[:, blk, :], in0=PTB[:, 0, :], scalar1=1.0, scalar2=None, op0=ALU.mult), reads=[PTB], writes=[BTK])
            elif bv == "none":
                pass
            p = nps()

            def trx():
                for q in range(4):
                    i = PE.transpose(p[:, q * 128:(q + 1) * 128], XC[:, q, bs], IDF)
                return i
            c.op("pe", trx, reads=[XC, CST], writes=[p])
            c.op("act", lambda: A_.activation(out=XCT[:, blk, :], in_=p[:], func=AF.Copy), reads=[p], writes=[XCT])
            c.op("dve", lambda: V.tensor_tensor(out=XDT[:, blk, :].rearrange("p (h e) -> p h e", e=64),
                                                in0=p[:].rearrange("p (h e) -> p h e", e=64),
                                                in1=DEC[:, blk, :].unsqueeze(2).to_broadcast([128, 8, 64]), op=ALU.mult),
                 reads=[p, DEC], writes=[XDT])
        for g in range(8):
            blk, a = g // 2, g % 2
            ps_ = slice(a * 64, a * 64 + 64)
            c.op("act", lambda: A_.activation(out=SBST[:, g, :], in_=SF[:], func=AF.Copy), reads=[SF], writes=[SBST])
            p = nps()
            c.op("pe", lambda: PE.matmul(p[:], BTK[ps_, blk, :], XDT[ps_, blk, :], start=True, stop=True), reads=[BTK, XDT], writes=[p])
            c.op("dve", lambda: V.tensor_tensor(out=SF[:].rearrange("p (h e) -> p h e", e=64), in0=SF[:].rearrange("p (h e) -> p h e", e=64),
                                                in1=CDS[:, :, g].unsqueeze(2).to_broadcast([128, 8, 64]), op=ALU.mult),
                 reads=[SF, CDS], writes=[SF])
            c.op("dve", lambda: V.tensor_tensor(out=SF[:], in0=SF[:], in1=p[:], op=ALU.add), reads=[SF, p], writes=[SF])
        PYq = PYQ
        for blk in range(4):
            bs = slice(blk * 128, (blk + 1) * 128)
            sg = SEG[blk % 2]; ml = MTL[blk % 2]
            p = nps()
            c.op("pe", lambda: PE.matmul(p[:, 0:128], BAb[:, bs], CAb[:, bs], start=True, stop=True), reads=[BAb, CAb], writes=[p])
            c.op("dve", lambda: V.tensor_tensor(out=sg[:], in0=ACSB[:, :, bs], in1=ACST[:, blk, :].unsqueeze(2).to_broadcast([128, 8, 128]),
                                                op=ALU.subtract), reads=[ACSB, ACST], writes=[sg])
            c.op("pool", lambda: G_.tensor_tensor(out=sg[:], in0=sg[:], in1=NEG2.unsqueeze(1).to_broadcast([128, 8, 128]), op=ALU.add),
                 reads=[sg, CST], writes=[sg])
            c.op("act", lambda: A_.activation(out=sg[:], in_=sg[:], func=AF.Exp), reads=[sg], writes=[sg])
            c.op("dve", lambda: V.tensor_tensor(out=ml[:], in0=sg[:], in1=p[:, 0:128].unsqueeze(1).to_broadcast([128, 8, 128]), op=ALU.mult),
                 reads=[sg, p], writes=[ml])
            for q in range(4):
                def ym():
                    i = None
                    for h2 in range(2):
                        h = 2 * q + h2
                        o = PYq[q][h2 * 64:h2 * 64 + 64, bs]
                        PE.matmul(o, XCT[:, blk, h * 64:(h + 1) * 64], ml[:, h, :], start=True, stop=False)
                        PE.matmul(PYq[q][h2 * 64:h2 * 64 + 64, blk * 128:blk * 128 + 64], SBST[:, 2 * blk, h * 64:(h + 1) * 64],
                                  CE[:, h, blk * 128:blk * 128 + 64], start=False, stop=False)
                        i = PE.matmul(PYq[q][h2 * 64:h2 * 64 + 64, blk * 128 + 64:blk * 128 + 128], SBST[:, 2 * blk + 1, h * 64:(h + 1) * 64],
                                      CE[:, h, blk * 128 + 64:blk * 128 + 128], start=False, stop=True)
                    return i
                c.op("pe", ym, reads=[XCT, ml, SBST, CE], writes=[PYq[q]])
        for q in range(4):
            t = tm()
            c.op("dve", lambda: V.scalar_tensor_tensor(out=Y2[:, q, :], in0=XSA[:, q, :], scalar=PAR[:, 74 + q:75 + q], in1=PYq[q][:],
                                                       op0=ALU.mult, op1=ALU.add), reads=[XSA, PAR, PYq[q]], writes=[Y2])
            c.op("act", lambda: A_.activation(out=t[:], in_=Z[:, q, :], func=AF.Silu), reads=[Z], writes=[t])
            c.op("pool", lambda: G_.tensor_tensor(out=Y2[:, q, :], in0=Y2[:, q, :], in1=t[:], op=ALU.mult), reads=[Y2, t], writes=[Y2])
            c.op("act", lambda: A_.activation(out=YSQ[:, q, :], in_=Y2[:, q, :], func=AF.Square), reads=[Y2], writes=[YSQ])
        p = nps()

        def mmn():
            for q in range(4):
                i = PE.matmul(p[:], ONESF[:], YSQ[:, q, :], start=(q == 0), stop=(q == 3))
            return i
        c.op("pe", mmn, reads=[ONESF, YSQ], writes=[p])
        c.op("act", lambda: A_.activation(out=RST[:], in_=p[:], func=AF.Sqrt, scale=1.0 / 512, bias=EPSB[:]), reads=[p, EPSB], writes=[RST])
        c.op("dve", lambda: V.reciprocal(out=RST[:], in_=RST[:]), reads=[RST], writes=[RST])
        for q in range(4):
            ob = OUTB[q % 2]
            c.op("dve", lambda: V.scalar_tensor_tensor(out=ob[:], in0=Y2[:, q, :], scalar=PAR[:, 78 + q:79 + q], in1=RST[:],
                                                       op0=ALU.mult, op1=ALU.mult), reads=[Y2, PAR, RST], writes=[ob])
            c.dma_out("sp", ob, yT[q * 128:(q + 1) * 128, ts], ob[:])
    print(f"[ssd] ops={c.nops} waits={c.nwaits}")


import numpy as np

NH = 16
HD = 128
NQ = 2048
HALO = 512
NK = NQ + HALO
WIN = 640
NEG = -30000.0


def attn_bias(rel_bias):
    qi = np.arange(128)[:, None]
    kj = np.arange(WIN)[None, :]
    rel = np.clip(512 + qi - kj, -63, 256) + 63
    a = qi // 64
    kc = kj // 64
    valid = (kc >= a) & (kc <= a + 8)
    out = rel_bias[:, rel]
    out = np.where(valid[None], out, np.float32(NEG)).astype(np.float32)
    return np.ascontiguousarray(out)


def halo_mask(th):
    m = np.zeros((128, 4, WIN), np.float32)
    if th == 0:
        for qt in range(4):
            m[:, qt, :HALO - 128 * qt] = NEG
    return np.ascontiguousarray(m.reshape(128, 4 * WIN))


def emit_attn(nc, c, io):
    qT = io["qT"]; kT = io["kT"]; vT = io["vT"]; bias = io["bias"]; hm = io["hm"]; idm = io["idm"]; oT = io["oT"]
    V, A_, G_, PE = nc.vector, nc.scalar, nc.gpsimd, nc.tensor
    scale = HD ** -0.5

    BIAS = c.sbuf("BIAS", [128, NH, WIN], F32)
    HM = c.sbuf("HM", [128, 4, WIN], F32)
    IDB = c.sbuf("IDB", [128, 128], BF16)
    QT = [c.sbuf(f"QT{i}", [128, NQ], BF16) for i in range(2)]
    KT = [c.sbuf(f"KT{i}", [128, NK], BF16) for i in range(2)]
    VT = [c.sbuf(f"VT{i}", [128, NK], BF16) for i in range(2)]
    VTOK = [c.sbuf(f"VTOK{i}", [128, NK // 128, 128], BF16) for i in range(2)]
    S = [c.sbuf(f"S{i}", [128, WIN], F32) for i in range(3)]
    PB = [c.sbuf(f"PB{i}", [128, WIN], BF16) for i in range(3)]
    PTS = [c.sbuf(f"PTS{i}", [128, 5, 128], BF16) for i in range(3)]
    MX = [c.sbuf(f"MX{i}", [128, 4], F32) for i in range(3)]
    OB = [c.sbuf(f"OB{i}", [128, 512], BF16) for i in range(2)]
    PS = [c.psum(f"PS{i}", [128, 1024], F32) for i in range(2)]
    PTP = [c.psum(f"PTP{i}", [128, 8, 128], BF16) for i in range(2)]
    PO = [c.psum(f"PO{i}", [128, 512], F32) for i in range(2)]

    c.dma_in("sp", BIAS, BIAS[:], bias.rearrange("h q k -> q h k"))
    c.dma_in("sp", HM, HM[:].rearrange("p a b -> p (a b)"), hm[:, :])
    c.dma_in("pool", IDB, IDB[:], idm[:, :])

    it = [0]
    for h in range(NH):
        hb = h % 2
        rows = slice(h * HD, (h + 1) * HD)
        c.dma_in("pool", QT[hb], QT[hb][:], qT[rows, :])
        c.dma_in("sp", KT[hb], KT[hb][:], kT[rows, :])
        c.dma_in("sp", VT[hb], VT[hb][:], vT[rows, :])
        for g in range(NK // 128 // 4):
            pt = PTP[g % 2]

            def tr():
                for b4 in range(4):
                    blk = g * 4 + b4
                    i = PE.transpose(pt[:, b4, :], VT[hb][:, blk * 128:(blk + 1) * 128], IDB[:])
                return i
            c.op("pe", tr, reads=[VT[hb], IDB], writes=[pt])
            c.op("act", lambda: A_.activation(out=VTOK[hb][:, g * 4:(g + 1) * 4, :], in_=pt[:, 0:4, :], func=AF.Copy),
                 reads=[pt], writes=[VTOK[hb]])
        for qt in range(NQ // 128):
            i3 = it[0] % 3
            it[0] += 1
            ps = PS[qt % 2]
            s = S[i3]; pb = PB[i3]; pts = PTS[i3]; mx = MX[i3]

            def sm():
                PE.matmul(ps[:, 0:512], QT[hb][:, qt * 128:(qt + 1) * 128], KT[hb][:, qt * 128:qt * 128 + 512], start=True, stop=True)
                return PE.matmul(ps[:, 512:640], QT[hb][:, qt * 128:(qt + 1) * 128], KT[hb][:, qt * 128 + 512:qt * 128 + 640],
                                 start=True, stop=True)
            c.op("pe", sm, reads=[QT[hb], KT[hb]], writes=[ps])
            c.op("dve", lambda: V.scalar_tensor_tensor(out=s[:], in0=ps[:, 0:WIN], scalar=scale, in1=BIAS[:, h, :],
                                                       op0=ALU.mult, op1=ALU.add), reads=[ps, BIAS], writes=[s])
            if qt < 4:
                c.op("pool", lambda: G_.tensor_tensor(out=s[:], in0=s[:], in1=HM[:, qt, :], op=ALU.add), reads=[s, HM], writes=[s])
            c.op("dve", lambda: V.reduce_max(out=mx[:, 0:1], in_=s[:], axis=AX.X), reads=[s], writes=[mx])
            c.op("dve", lambda: V.tensor_scalar(out=mx[:, 1:2], in0=mx[:, 0:1], scalar1=-1.0, scalar2=None, op0=ALU.mult),
                 reads=[mx], writes=[mx])
            c.op("act", lambda: A_.activation(out=s[:], in_=s[:], func=AF.Exp, bias=mx[:, 1:2], accum_out=mx[:, 2:3]),
                 reads=[s, mx], writes=[s, mx])
            c.op("dve", lambda: V.reciprocal(out=mx[:, 3:4], in_=mx[:, 2:3]), reads=[mx], writes=[mx])
            c.op("dve", lambda: V.tensor_scalar(out=pb[:], in0=s[:], scalar1=mx[:, 3:4], scalar2=None, op0=ALU.mult),
                 reads=[s, mx], writes=[pb])
            pt = PTP[qt % 2]

            def tp():
                for b5 in range(5):
                    i = PE.transpose(pt[:, b5, :], pb[:, b5 * 128:(b5 + 1) * 128], IDB[:])
                return i
            c.op("pe", tp, reads=[pb, IDB], writes=[pt])
            c.op("act", lambda: A_.activation(out=pts[:], in_=pt[:, 0:5, :], func=AF.Copy), reads=[pt], writes=[pts])
            po = PO[(qt // 4) % 2]

            def pv():
                for b5 in range(5):
                    i = PE.matmul(po[:, (qt % 4) * 128:(qt % 4 + 1) * 128], VTOK[hb][:, qt + b5, :], pts[:, b5, :],
                                  start=(b5 == 0), stop=(b5 == 4))
                return i
            c.op("pe", pv, reads=[VTOK[hb], pts], writes=[po])
            if qt % 4 == 3:
                ob = OB[(qt // 4) % 2]
                c.op("act", lambda: A_.activation(out=ob[:], in_=po[:], func=AF.Copy), reads=[po], writes=[ob])
                c.dma_out("sp", ob, oT[rows, (qt // 4) * 512:(qt // 4 + 1) * 512], ob[:])
    print(f"[attn] ops={c.nops} waits={c.nwaits}")


from concourse.bass_utils import run_bass_kernel_spmd

I32 = mybir.dt.int32
_CACHE = {}
T_SEQ = 4096
NTH = 2048


def build_fused():
    nc = bass.Bass("TRN2", target_bir_lowering=False)
    T = T_SEQ

    def ext(name, shape, dt=F32):
        return nc.dram_tensor(name, list(shape), dt, kind="ExternalInput").ap()
    xT = ext("xT", [D, T]); xTh = ext("xTh", [D, NTH]); cid = ext("cid", [1, 8], I32)
    g_a = ext("g_a", [128, 16]); g_b = ext("g_b", [128, 64]); g_c = ext("g_c", [128, 48])
    w_in = ext("w_in", [NCT, 128, D])
    par = ext("par", [128, 82]); hpar = ext("hpar", [8, 2]); lora = ext("lora", [128, 2048])
    rcst = ext("rcst", [128, 1152]); scst = ext("scst", [128, 896]); sel = ext("sel", [8, 1536])
    w_o0 = ext("w_o0", [DC, 128, D]); w_g0 = ext("w_g0", [FC, 128, D]); w_u0 = ext("w_u0", [FC, 128, D])
    w_d0 = ext("w_d0", [DC, 128, DFF]); w_qkv = ext("w_qkv", [48, 128, D])
    w_o1 = ext("w_o1", [DC, 128, D]); w_g1 = ext("w_g1", [FC, 128, D]); w_u1 = ext("w_u1", [FC, 128, D])
    w_d1 = ext("w_d1", [DC, 128, DFF])
    abias = ext("abias", [NH, 128, WIN]); hm = ext("hm", [128, 4 * WIN]); idm = ext("idm", [128, 128])
    xo = nc.dram_tensor("xo", [D, NTH], F32, kind="ExternalOutput").ap()
    po = nc.dram_tensor("po_i", [NCT * 128, T], F32).ap()
    ymy_t = nc.dram_tensor("ymy", [1024, T], BF16)
    yall_t = nc.dram_tensor("yall", [8 * 1024, T], BF16)
    ymix = nc.dram_tensor("ymix", [D, NTH], BF16).ap()
    x1 = nc.dram_tensor("x1_i", [D, NTH], F32).ap()
    qkv = nc.dram_tensor("qkv_i", [3 * D, NTH], F32).ap()
    kvh_t = nc.dram_tensor("kvh", [2 * D, HALO], BF16)
    kvall_t = nc.dram_tensor("kvall", [8 * 2 * D, HALO], BF16)
    kTi = nc.dram_tensor("kT_i", [D, NK], BF16).ap()
    vTi = nc.dram_tensor("vT_i", [D, NK], BF16).ap()
    oTi = nc.dram_tensor("oT_i", [D, NTH], BF16).ap()
    ymy = ymy_t.ap(); yall = yall_t.ap(); kvh = kvh_t.ap(); kvall = kvall_t.ap()

    c = Ctx(nc)
    g = nc.gpsimd
    cc_sem = c.stack.enter_context(nc.semaphore("cc_sem"))
    xs_sem = c.stack.enter_context(nc.semaphore("xs_sem"))
    CID = c.stack.enter_context(nc.sbuf_tensor("CIDS", [1, 8], I32))
    DUM = c.stack.enter_context(nc.sbuf_tensor("DUMMY", [1, 8], F32))
    regs = [c.stack.enter_context(g.register(f"cidr{i}")) for i in range(4)]
    xs_cnt = [0]
    cc_cnt = [0]

    def xdma(out, in_):
        g.dma_start(out=out, in_=in_).then_inc(xs_sem, 16)
        xs_cnt[0] += 16

    def xwait():
        g.wait_ge(xs_sem, xs_cnt[0])

    class _DumBuf:
        pass

    def sync_point():
        fake = _DumBuf(); fake.writes = {}; fake.reads = {}; fake.space = "sbuf"; fake.name = "dummy"
        c.op("pool", lambda: g.memset(DUM[:], 0.0), writes=[fake])
        c.barrier()

    c.prefix = "a_"
    xdma(CID[:, :], cid[:, :])
    emit_dense(nc, c, {"xT": xT, "gains": g_a, "w_p": w_in, "po": po}, T, False, False, NCT, False, name="inproj")
    c.end_phase("r_")
    emit_rwkv(nc, c, {"pT": po[0:16 * 128, :], "par": par, "lora": lora, "cst": rcst, "yT": ymy[0:512, :]}, T)
    c.end_phase("s_")
    emit_ssd(nc, c, {"pT": po[16 * 128:NCT * 128, :], "par": par, "hpar": hpar, "cst": scst, "sel": sel, "yT": ymy[512:1024, :]}, T)
    c.end_phase("b_")
    xwait()
    g.collective_compute("AllGather", ALU.bypass, replica_groups=[list(range(8))],
                         ins=[ymy_t.ap().opt()], outs=[yall_t.ap().opt()]).then_inc(cc_sem)
    cc_cnt[0] += 1
    g.wait_ge(cc_sem, cc_cnt[0])
    for i in range(4):
        g.reg_load(regs[i], CID[0:1, i:i + 1])
    r0 = g.snap(regs[0]); r1 = g.snap(regs[1]); sv = g.snap(regs[2]); hr = g.snap(regs[3])
    yv = yall.rearrange("(k r) (h t) -> k r h t", r=1024, h=2)
    for qi, (rv, lo) in enumerate([(r0, 0), (r1, 0), (r0, 512), (r1, 512)]):
        src = yv[bass.ds(rv, 1), lo:lo + 512, bass.ds(sv, 1), :].rearrange("k r h t -> (k r) (h t)")
        xdma(ymix[qi * 512:(qi + 1) * 512, :], src)
    xwait()
    sync_point()
    emit_dense(nc, c, {"xT": xTh, "gains": g_b, "yT": ymix, "w_o": w_o0, "w_g": w_g0, "w_u": w_u0, "w_d": w_d0,
                       "w_p": w_qkv, "po": qkv, "xo": x1}, NTH, True, True, 48, True, name="block0")
    c.end_phase("c_")
    xdma(kvh[0:D, :], qkv[D:2 * D, NTH - HALO:NTH])
    xdma(kvh[D:2 * D, :], qkv[2 * D:3 * D, NTH - HALO:NTH])
    xdma(kTi[:, HALO:NK], qkv[D:2 * D, :])
    xdma(vTi[:, HALO:NK], qkv[2 * D:3 * D, :])
    xwait()
    g.collective_compute("AllGather", ALU.bypass, replica_groups=[list(range(8))],
                         ins=[kvh_t.ap().opt()], outs=[kvall_t.ap().opt()]).then_inc(cc_sem)
    cc_cnt[0] += 1
    g.wait_ge(cc_sem, cc_cnt[0])
    kvv = kvall.rearrange("(k r) t -> k r t", r=2 * D)
    xdma(kTi[:, 0:HALO], kvv[bass.ds(hr, 1), 0:D, :].rearrange("k r t -> (k r) t"))
    xdma(vTi[:, 0:HALO], kvv[bass.ds(hr, 1), D:2 * D, :].rearrange("k r t -> (k r) t"))
    xwait()
    sync_point()
    emit_attn(nc, c, {"qT": qkv[0:D, :], "kT": kTi, "vT": vTi, "bias": abias, "hm": hm, "idm": idm, "oT": oTi})
    c.end_phase("d_")
    emit_dense(nc, c, {"xT": x1, "gains": g_c, "yT": oTi, "w_o": w_o1, "w_g": w_g1, "w_u": w_u1, "w_d": w_d1, "xo": xo},
               NTH, True, True, 0, True, name="block1")
    c.finish()
    c.close()
    return nc


def kernel(**inputs):
    f32 = np.float32
    I = {k: np.asarray(v) for k, v in inputs.items()}
    x = I["x"].astype(f32, copy=False)
    ng = I["norm_g"]
    B, T, Dm = x.shape
    L0 = {k: I[k][0] for k in I if k.startswith("rwkv") or k.startswith("ssm")}
    w_in = I["w_in_ab"][0]
    if "nc" not in _CACHE:
        _CACHE["nc"] = build_fused()
    nc = _CACHE["nc"]
    wp = [tile_w(gather_cols(w_in, half_cols(hh))) for hh in range(2)]
    mp = [mixer_params(hh, L0) for hh in range(2)]
    sc, ssel = ssd_consts()
    shared = {
        "g_a": gains_layout([ng[0, 0]]),
        "g_b": gains_layout([ng[0, 1], ng[0, 2], ng[0, 3], ng[1, 0]]),
        "g_c": gains_layout([ng[1, 1], ng[1, 2], ng[1, 3]]),
        "rcst": rwkv_consts(), "scst": sc, "sel": ssel,
        "w_o0": tile_w(I["w_out_ab"][0]), "w_g0": tile_w(I["ffn_w_gate"][0]), "w_u0": tile_w(I["ffn_w_up"][0]),
        "w_d0": tile_w(I["ffn_w_down"][0]), "w_qkv": tile_w(I["w_qkv"][0]),
        "w_o1": tile_w(I["w_out_c"][0]), "w_g1": tile_w(I["ffn_w_gate"][1]), "w_u1": tile_w(I["ffn_w_up"][1]),
        "w_d1": tile_w(I["ffn_w_down"][1]),
        "abias": attn_bias(I["attn_rel_bias"][0]), "idm": np.eye(128, dtype=f32),
    }
    xTb = [np.ascontiguousarray(x[b].T) for b in range(B)]
    in_maps = []
    for c in range(8):
        b, s = c // 2, c % 2
        d = dict(shared)
        d["xT"] = xTb[b]
        d["xTh"] = np.ascontiguousarray(xTb[b][:, s * NTH:(s + 1) * NTH])
        d["cid"] = np.array([[2 * b, 2 * b + 1, s, (c - 1) if s == 1 else c, 0, 0, 0, 0]], np.int32)
        d["w_in"] = wp[s]
        d["par"] = mp[s]["par"]; d["hpar"] = mp[s]["hpar"]; d["lora"] = mp[s]["lora"]
        d["hm"] = halo_mask(s)
        in_maps.append(d)
    res = run_bass_kernel_spmd(nc, in_maps, core_ids=list(range(8)))
    out = np.empty((B, T, Dm), f32)
    for c in range(8):
        b, s = c // 2, c % 2
        out[b, s * NTH:(s + 1) * NTH, :] = res.results[c]["xo"].T
    return out
```

```python
import contextlib
import numpy as np
import concourse.bass as bass
import concourse.mybir as mybir

F32 = mybir.dt.float32
BF16 = mybir.dt.bfloat16
AF = mybir.ActivationFunctionType
ALU = mybir.AluOpType
AX = mybir.AxisListType


class Buf:
    def __init__(self, ctx, name, shape, dtype, space="sbuf"):
        self.ctx = ctx
        self.name = name
        self.shape = shape
        self.dtype = dtype
        name = ctx.prefix + name
        self.name = name
        if space == "sbuf":
            cm = ctx.nc.sbuf_tensor(name, list(shape), dtype)
        else:
            cm = ctx.nc.psum_tensor(name, list(shape), dtype)
        self.t = ctx.pstack.enter_context(cm)
        ctx.phase_bufs.append(self)
        self.space = space
        self.writes = {}
        self.reads = {}
        self._dsem = None
        self._rsem = None
        self.dcount = 0
        self.rcount = 0

    def __getitem__(self, idx):
        return self.t[idx]

    def dsem(self):
        if self._dsem is None:
            self._dsem, self.dcount = self.ctx.pool_sem()
        return self._dsem

    def rsem(self):
        if self._rsem is None:
            self._rsem, self.rcount = self.ctx.pool_sem()
        return self._rsem


class Ctx:
    ENG = ("pe", "dve", "act", "pool", "sp")

    def __init__(self, nc):
        self.nc = nc
        self.stack = contextlib.ExitStack()
        self.pstack = contextlib.ExitStack()
        self.phase_bufs = []
        self.prefix = ""
        self.free_sems = []
        self.dma_sems = {}
        self.nsem = 0
        self.eng = {"pe": nc.tensor, "dve": nc.vector, "act": nc.scalar,
                    "pool": nc.gpsimd, "sp": nc.sync}
        self.sems = {}
        self.semobj = {}
        self.cnt = {}
        self.known = {e: {} for e in self.ENG}
        for e in self.ENG:
            self.semobj[e] = self.stack.enter_context(nc.semaphore("e_" + e))
            self.cnt[e] = 0
        self.nwaits = 0
        self.nops = 0
        import os
        self.limit = int(os.environ.get("MK_LIMIT", "1000000000"))
        self.log = []

    def new_sem(self, name):
        s = self.stack.enter_context(self.nc.semaphore(name))
        self.semobj[name] = s
        return name

    def pool_sem(self):
        if self.free_sems:
            return self.free_sems.pop()
        self.nsem += 1
        return self.new_sem(f"dq{self.nsem}"), 0

    def barrier(self):
        for b in self.phase_bufs:
            if b._dsem is not None:
                self.dma_sems[b._dsem] = max(self.dma_sems.get(b._dsem, 0), b.dcount)
            if b._rsem is not None:
                self.dma_sems[b._rsem] = max(self.dma_sems.get(b._rsem, 0), b.rcount)
        deps = [(e, self.cnt[e]) for e in self.ENG if self.cnt[e] > 0] + list(self.dma_sems.items())
        for e in self.ENG:
            self._wait(e, deps)

    def end_phase(self, prefix=""):
        self.barrier()
        for b in self.phase_bufs:
            if b._dsem is not None:
                self.free_sems.append((b._dsem, b.dcount))
            if b._rsem is not None:
                self.free_sems.append((b._rsem, b.rcount))
        self.phase_bufs = []
        self.pstack.close()
        self.pstack = contextlib.ExitStack()
        self.prefix = prefix

    def close(self):
        self.pstack.close()
        self.stack.close()

    def sbuf(self, name, shape, dtype):
        return Buf(self, name, shape, dtype, "sbuf")

    def psum(self, name, shape, dtype=F32):
        return Buf(self, name, shape, dtype, "psum")

    def _wait(self, e, deps):
        k = self.known[e]
        need = {}
        for (s, v) in deps:
            if k.get(s, 0) < v and need.get(s, 0) < v:
                need[s] = v
        for s, v in need.items():
            self.eng[e].wait_ge(self.semobj[s], v)
            k[s] = v
            self.nwaits += 1

    def _deps(self, reads, writes):
        deps = []
        for b in reads:
            deps += list(b.writes.items())
            if b.space == "psum":
                deps += list(b.reads.items())
        for b in writes:
            deps += list(b.writes.items())
            deps += list(b.reads.items())
        return deps

    def op(self, e, fn, reads=(), writes=()):
        if self.nops >= self.limit:
            return None
        import traceback
        self.log.append((self.nops, e, traceback.extract_stack(limit=3)[0].lineno, [b.name for b in writes]))
        self._wait(e, self._deps(reads, writes))
        inst = fn()
        inst.then_inc(self.semobj[e], 1)
        self.cnt[e] += 1
        ev = (e, self.cnt[e])
        for b in writes:
            b.writes = {ev[0]: ev[1]}
            b.reads = {}
        for b in reads:
            if b not in writes:
                b.reads[ev[0]] = ev[1]
        self.nops += 1
        return inst

    def dma_in(self, q, dst_buf, dst_ap, src_ap, first=True, **kw):
        if self.nops >= self.limit:
            return
        if first:
            dst_buf.war = self._deps((), (dst_buf,))
        self._wait(q, dst_buf.war)
        s = dst_buf.dsem()
        self.eng[q].dma_start(out=dst_ap, in_=src_ap, **kw).then_inc(self.semobj[s], 16)
        dst_buf.dcount += 16
        dst_buf.writes = {s: dst_buf.dcount}
        dst_buf.reads = {}

    def dma_out(self, q, src_buf, dst_ap, src_ap, **kw):
        if self.nops >= self.limit:
            return
        self._wait(q, self._deps((src_buf,), ()))
        s = src_buf.rsem()
        self.eng[q].dma_start(out=dst_ap, in_=src_ap, **kw).then_inc(self.semobj[s], 16)
        src_buf.rcount += 16
        src_buf.reads[s] = src_buf.rcount
        self.dma_sems[s] = src_buf.rcount

    def finish(self):
        self.barrier()


import numpy as np

D = 2048
DC = D // 128
DFF = 5632
FC = DFF // 128
TT = 512
EPS = 1e-6


class WStream:
    def __init__(self, c, nbuf=8):
        self.c = c
        self.bufs = [c.sbuf(f"wst{i}", [128, 2048], BF16) for i in range(nbuf)]
        self.items = []
        self.issued = 0
        self.taken = 0

    def add(self, ap):
        self.items.append(ap)
        return len(self.items) - 1

    def _issue_upto(self, n):
        n = min(n, len(self.items))
        while self.issued < n:
            i = self.issued
            b = self.bufs[i % len(self.bufs)]
            ap = self.items[i]
            ne = ap.shape[-1]
            self.c.dma_in("pool", b, b[:, 0:ne], ap)
            self.issued += 1

    def take(self):
        i = self.taken
        self._issue_upto(i + len(self.bufs) - 2)
        if self.issued <= i:
            self._issue_upto(i + 1)
        self.taken += 1
        return self.bufs[i % len(self.bufs)]


def tile_w(W):
    K, N = W.shape
    Np = -(-N // 128) * 128
    if Np != N:
        W = np.concatenate([W, np.zeros((K, Np - N), W.dtype)], axis=1)
    return np.ascontiguousarray(
        W.reshape(K // 128, 128, Np // 128, 128).transpose(2, 1, 0, 3).reshape(Np // 128, 128, K))


def gains_layout(gs):
    return np.ascontiguousarray(
        np.stack([g.reshape(DC, 128).T for g in gs], axis=1).reshape(128, len(gs) * DC)).astype(np.float32)


def emit_dense(nc, c, io, NT, has_mix, has_ffn, n_proj_ct, write_x, name="dense"):
    ng = (1 if has_mix else 0) + (2 if has_ffn else 0) + (1 if n_proj_ct else 0)
    xT = io["xT"]; gains = io["gains"]
    if has_mix:
        yT = io["yT"]; w_o = io["w_o"]
    if has_ffn:
        w_g = io["w_g"]; w_u = io["w_u"]; w_d = io["w_d"]
    if n_proj_ct:
        w_p = io["w_p"]; po = io["po"]
    if write_x:
        xo = io["xo"]
    X = c.sbuf("X", [128, DC, TT], F32)
    MT = c.sbuf("MT", [128, DC, TT], F32)
    HT = c.sbuf("HT", [128, DC, TT], BF16)
    HID = c.sbuf("HID", [128, FC, TT], BF16)
    RSTD = c.sbuf("RSTD", [128, TT], F32)
    TMP = [c.sbuf(f"TMP{i}", [128, TT], F32) for i in range(3)]
    G = c.sbuf("G", [128, ng * DC], F32)
    ONES = c.sbuf("ONES", [128, 128], BF16)
    EPSB = c.sbuf("EPSB", [128, 1], F32)
    PS = [c.psum(f"PS{i}", [128, TT], F32) for i in range(8)]
    ws = WStream(c, nbuf=8)
    psi = [0]

    def nextps():
        p = PS[psi[0] % 8]
        psi[0] += 1
        return p

    c.dma_in("sp", G, G[:], gains[:, :])
    c.op("dve", lambda: nc.vector.memset(ONES[:], 1.0), writes=[ONES])
    c.op("dve", lambda: nc.vector.memset(EPSB[:], EPS), writes=[EPSB])

    xT_v = xT.rearrange("(c p) t -> p c t", p=128)
    if has_mix:
        yT_v = yT.rearrange("(c p) t -> p c t", p=128)
    if write_x:
        xo_v = xo.rearrange("(c p) t -> p c t", p=128)

    ntiles = NT // TT
    for j in range(ntiles):
        if has_mix:
            for dc in range(DC):
                ws.add(w_o[dc])
        if has_ffn:
            for fc in range(FC):
                ws.add(w_g[fc])
                ws.add(w_u[fc])
            for dc in range(DC):
                ws.add(w_d[dc, :, 0:2048])
                ws.add(w_d[dc, :, 2048:4096])
                ws.add(w_d[dc, :, 4096:5632])
        for ct in range(n_proj_ct):
            ws.add(w_p[ct])

    tmpi = [0]

    def nexttmp():
        t = TMP[tmpi[0] % 3]
        tmpi[0] += 1
        return t

    def rstd_from_sq():
        p = nextps()

        def mm():
            for k in range(DC):
                i = nc.tensor.matmul(p[:], ONES[:], HID[:, k, :], start=(k == 0), stop=(k == DC - 1))
            return i
        c.op("pe", mm, reads=[ONES, HID], writes=[p])
        c.op("act", lambda: nc.scalar.activation(out=RSTD[:], in_=p[:], func=AF.Sqrt, scale=1.0 / D, bias=EPSB[:]),
             reads=[p, EPSB], writes=[RSTD])
        c.op("dve", lambda: nc.vector.reciprocal(out=RSTD[:], in_=RSTD[:]), reads=[RSTD], writes=[RSTD])

    def norm_to_HT(gi):
        for k in range(DC):
            c.op("act", lambda k=k: nc.scalar.activation(out=HID[:, k, :], in_=X[:, k, :], func=AF.Square),
                 reads=[X], writes=[HID])
        rstd_from_sq()
        for k in range(DC):
            c.op("dve", lambda k=k: nc.vector.scalar_tensor_tensor(
                out=HT[:, k, :], in0=X[:, k, :], scalar=G[:, gi * DC + k:gi * DC + k + 1], in1=RSTD[:],
                op0=ALU.mult, op1=ALU.mult), reads=[X, G, RSTD], writes=[HT])

    def add_normed_MT(gi):
        rstd_from_sq()
        c.op("dve", lambda: nc.vector.tensor_tensor(out=MT[:], in0=MT[:], in1=RSTD[:].unsqueeze(1).to_broadcast([128, DC, TT]),
                                                    op=ALU.mult), reads=[MT, RSTD], writes=[MT])
        for k in range(DC):
            c.op("dve", lambda k=k: nc.vector.scalar_tensor_tensor(
                out=X[:, k, :], in0=MT[:, k, :], scalar=G[:, gi * DC + k:gi * DC + k + 1], in1=X[:, k, :],
                op0=ALU.mult, op1=ALU.add), reads=[MT, G, X], writes=[X])

    def proj_chunk(rhs_buf, nk, wbufs, koffs):
        p = nextps()

        def mm():
            k = 0
            i = None
            for (wb, n) in wbufs:
                for kk in range(n):
                    i = nc.tensor.matmul(p[:], wb[:, kk * 128:(kk + 1) * 128], rhs_buf[:, k, :],
                                         start=(k == 0), stop=(k == nk - 1))
                    k += 1
            return i
        c.op("pe", mm, reads=[rhs_buf] + [w for w, _ in wbufs], writes=[p])
        return p

    for j in range(ntiles):
        ts = slice(j * TT, (j + 1) * TT)
        c.dma_in("sp", X, X[:], xT_v[:, :, ts])
        gi = 0
        if has_mix:
            c.dma_in("sp", HT, HT[:], yT_v[:, :, ts])
            for dc in range(DC):
                wb = ws.take()
                p = proj_chunk(HT, DC, [(wb, DC)], None)
                c.op("act", lambda dc=dc, p=p: nc.scalar.activation(out=MT[:, dc, :], in_=p[:], func=AF.Copy),
                     reads=[p], writes=[MT])
                c.op("act", lambda dc=dc, p=p: nc.scalar.activation(out=HID[:, dc, :], in_=p[:], func=AF.Square),
                     reads=[p], writes=[HID])
            add_normed_MT(gi)
            gi += 1
        if has_ffn:
            norm_to_HT(gi)
            gi += 1
            for fc in range(FC):
                wg = ws.take()
                pg = proj_chunk(HT, DC, [(wg, DC)], None)
                wu = ws.take()
                pu = proj_chunk(HT, DC, [(wu, DC)], None)
                t = nexttmp()
                c.op("act", lambda t=t, pg=pg: nc.scalar.activation(out=t[:], in_=pg[:], func=AF.Silu),
                     reads=[pg], writes=[t])
                c.op("dve", lambda fc=fc, t=t, pu=pu: nc.vector.tensor_tensor(out=HID[:, fc, :], in0=t[:], in1=pu[:], op=ALU.mult),
                     reads=[t, pu], writes=[HID])
            for dc in range(DC):
                w0 = ws.take()
                w1 = ws.take()
                w2 = ws.take()
                p = proj_chunk(HID, FC, [(w0, 16), (w1, 16), (w2, 12)], None)
                c.op("act", lambda dc=dc, p=p: nc.scalar.activation(out=MT[:, dc, :], in_=p[:], func=AF.Copy),
                     reads=[p], writes=[MT])
            for dc in range(DC):
                c.op("act", lambda dc=dc: nc.scalar.activation(out=HID[:, dc, :], in_=MT[:, dc, :], func=AF.Square),
                     reads=[MT], writes=[HID])
            add_normed_MT(gi)
            gi += 1
        if write_x:
            c.dma_out("sp", X, xo_v[:, :, ts], X[:])
        if n_proj_ct:
            norm_to_HT(gi)
            for ct in range(n_proj_ct):
                wb = ws.take()
                p = proj_chunk(HT, DC, [(wb, DC)], None)
                t = nexttmp()
                if ct % 2 == 0:
                    c.op("act", lambda t=t, p=p: nc.scalar.activation(out=t[:], in_=p[:], func=AF.Copy),
                         reads=[p], writes=[t])
                else:
                    c.op("dve", lambda t=t, p=p: nc.vector.tensor_copy(out=t[:], in_=p[:]), reads=[p], writes=[t])
                c.dma_out("sp", t, po[ct * 128:(ct + 1) * 128, ts], t[:])
    print(f"[{name}] ops={c.nops} waits={c.nwaits}")


import numpy as np

TT = 512
NCH = 8
C = 64
NCT = 27
RW = 1024
RWKV_PROJ = 3520
LW_SCALE = 0.6065306597126334


def half_cols(hh):
    cols = []
    def seg(a, n, pad):
        cols.extend(list(range(a, a + n)) + [-1] * (pad - n))
    seg(0 + hh * 512, 512, 512)
    seg(1024 + hh * 512, 512, 512)
    seg(2048 + hh * 512, 512, 512)
    seg(3072, 96, 128)
    seg(3168, 96, 128)
    seg(3264, 256, 256)
    o = RWKV_PROJ
    seg(o + hh * 512, 512, 512)
    seg(o + 1024 + hh * 512, 512, 512)
    seg(o + 2048 + hh * 128, 128, 128)
    seg(o + 2304 + hh * 128, 128, 128)
    seg(o + 2560 + hh * 8, 8, 128)
    return np.array(cols)


def gather_cols(W, cols):
    out = np.zeros((W.shape[0], len(cols)), W.dtype)
    m = cols >= 0
    out[:, m] = W[:, cols[m]]
    return out


def fm(v, n=128):
    return np.ascontiguousarray(v.reshape(-1, n).T)


def mixer_params(hh, I):
    f32 = np.float32
    cols = half_cols(hh)
    mu = np.zeros(16 * 128, f32)
    rc = cols[:16 * 128]
    mu[rc >= 0] = I["rwkv_mu"][rc[rc >= 0]]
    hs = slice(hh * 512, (hh + 1) * 512)
    P = {}
    par = [fm(mu)]
    par.append(fm(I["rwkv_w0"][hs]))
    par.append(fm(I["rwkv_a0"][hs]))
    par.append(fm(I["rwkv_k_k"][hs]))
    par.append(fm(I["rwkv_k_a"][hs]))
    par.append(fm(I["rwkv_r_k"].reshape(-1)[hs]))
    par.append(fm(I["rwkv_ln_w"][hs]))
    par.append(fm(I["rwkv_ln_b"][hs]))
    cw = I["ssm_conv_w"]
    cb = I["ssm_conv_b"]
    cidx = np.concatenate([np.arange(hh * 512, hh * 512 + 512), 1024 + hh * 128 + np.arange(128),
                           1280 + hh * 128 + np.arange(128)])
    for j in range(4):
        par.append(fm(cw[j][cidx]))
    par.append(fm(cb[cidx]))
    par.append(fm(np.repeat(I["ssm_D"][hh * 8:(hh + 1) * 8], 64)))
    par.append(fm(I["ssm_norm_w"][hs]))
    P["par"] = np.concatenate(par, axis=1).astype(f32)
    hp = np.zeros((8, 2), f32)
    hp[:, 0] = I["ssm_dt_bias"][hh * 8:(hh + 1) * 8]
    hp[:, 1] = I["ssm_A_log"][hh * 8:(hh + 1) * 8]
    P["hpar"] = hp
    w2 = np.zeros((128, 512), f32); w2[:96] = I["rwkv_w2"][:, hs]
    a2 = np.zeros((128, 512), f32); a2[:96] = I["rwkv_a2"][:, hs]
    g2 = I["rwkv_g2"][:, hs]
    P["lora"] = np.ascontiguousarray(np.concatenate([w2, a2, g2[:128], g2[128:]], axis=1)).astype(f32)
    return P


def rwkv_consts():
    f32 = np.float32
    p = np.arange(128)[:, None]
    j = np.arange(128)[None, :]
    idf = (p == j).astype(f32)
    msu = (j > p).astype(f32)
    miu = (j >= p).astype(f32)
    msl = (j < p).astype(f32)
    ob = ((p // 64) == (j // 64)).astype(f32)
    reset = np.ones((128, 512), f32)
    reset[:, ::64] = 0.0
    return np.ascontiguousarray(np.concatenate([idf, msu, miu, msl, ob, reset], axis=1))


def emit_rwkv(nc, c, io, T, tile_hook=None):
    pT = io["pT"]; par = io["par"]; lora = io["lora"]; cst = io["cst"]; yT = io["yT"]
    V, A_, G_, P_, PE = nc.vector, nc.scalar, nc.gpsimd, nc.gpsimd, nc.tensor

    PAR = c.sbuf("PAR", [128, 82], F32)
    OMKA = c.sbuf("OMKA", [128, 4], F32)
    LORA = c.sbuf("LORA", [128, 2048], BF16)
    CST = c.sbuf("CST", [128, 1152], F32)
    IDB = c.sbuf("IDB", [128, 128], BF16)
    ONESB = c.sbuf("ONESB", [128, 128], BF16)
    EPSL = c.sbuf("EPSL", [128, 1], F32)
    PR = c.sbuf("PR", [128, 16, 513], F32)
    CAR = c.sbuf("CAR", [128, 16], F32)
    TW = c.sbuf("TW", [128, 512], BF16)
    ACb = c.sbuf("ACb", [128, 512], BF16)
    SG = c.sbuf("SG", [128, 2, 512], BF16)
    NT_ = 14
    TM = [c.sbuf(f"TM{i}", [128, 512], F32) for i in range(NT_)]
    TB = [c.sbuf(f"TBF{i}", [128, 512], BF16) for i in range(3)]
    GATE = [c.sbuf(f"GATE{i}", [128, 512], F32) for i in range(4)]
    BON = [c.sbuf(f"BON{i}", [128, 512], F32) for i in range(4)]
    GAM = [c.sbuf(f"GAM{i}", [128, 8], F32) for i in range(4)]
    ARd = [c.sbuf(f"ARd{i}", [128, 8, 2, 128], BF16) for i in range(4)]
    BKd = [c.sbuf(f"BKd{i}", [128, 8, 2, 128], BF16) for i in range(4)]
    BHd = [c.sbuf(f"BHd{i}", [128, 8, 128], BF16) for i in range(4)]
    KHd = [c.sbuf(f"KHd{i}", [128, 8, 128], BF16) for i in range(4)]
    VDd = [c.sbuf(f"VDd{i}", [128, 8, 128], BF16) for i in range(4)]
    HF = [c.sbuf(f"HF{i}", [128, 128], F32) for i in range(4)]
    HB = [c.sbuf(f"HB{i}", [128, 128], BF16) for i in range(4)]
    NR = 4
    TOK = [c.sbuf(f"TOK{i}", [128, 4, 128], BF16) for i in range(NR)]
    X0 = [c.sbuf(f"X0_{i}", [128, 2, 128], BF16) for i in range(NR)]
    XA = [c.sbuf(f"XA{i}", [128, 3, 128], BF16) for i in range(NR)]
    XB = [c.sbuf(f"XB{i}", [128, 3, 128], BF16) for i in range(NR)]
    NA = [c.sbuf(f"NA{i}", [128, 2, 128], BF16) for i in range(NR)]
    NB0 = [c.sbuf(f"NB0_{i}", [128, 128], BF16) for i in range(NR)]
    NB1 = [c.sbuf(f"NB1_{i}", [128, 128], BF16) for i in range(NR)]
    ARB = [c.sbuf(f"ARB{i}", [128, 128], BF16) for i in range(NR)]
    ARK = [c.sbuf(f"ARK{i}", [128, 128], BF16) for i in range(NR)]
    TTb = [c.sbuf(f"TTb{i}", [128, 128], BF16) for i in range(NR)]
    PW = [c.sbuf(f"PW{i}", [128, 2, 128], BF16) for i in range(NR)]
    UB = [c.sbuf(f"UB{i}", [128, 128], BF16) for i in range(NR)]
    YFq = [c.sbuf(f"YF{i}", [128, 512], F32) for i in range(4)]
    OUTB = [c.sbuf(f"OUTB{i}", [128, 512], BF16) for i in range(2)]
    PA = c.psum("PA", [128, 512], F32)
    PB = c.psum("PB", [128, 512], F32)
    PI = [c.psum(f"PI{i}", [128, 512], F32) for i in range(4)]
    PU = c.psum("PU", [128, 512], F32)
    PT = c.psum("PT", [128, 4, 128], BF16)

    c.dma_in("sp", PAR, PAR[:], par[:, :])
    c.dma_in("sp", CST, CST[:], cst[:, :])
    c.dma_in("pool", LORA, LORA[:], lora[:, :])
    IDF = CST[:, 0:128]; MSU = CST[:, 128:256]; MIU = CST[:, 256:384]; MSL = CST[:, 384:512]
    RESET = CST[:, 640:1152]
    c.op("dve", lambda: V.tensor_copy(out=IDB[:], in_=CST[:, 0:128]), reads=[CST], writes=[IDB])
    c.op("dve", lambda: V.tensor_copy(out=ONESB[:], in_=CST[:, 512:640]), reads=[CST], writes=[ONESB])
    c.op("dve", lambda: V.memset(EPSL[:], 64e-5), writes=[EPSL])
    c.op("dve", lambda: V.tensor_scalar(out=OMKA[:], in0=PAR[:, 28:32], scalar1=-1.0, scalar2=1.0, op0=ALU.mult, op1=ALU.add),
         reads=[PAR], writes=[OMKA])
    c.op("pool", lambda: G_.memset(PR[:], 0.0), writes=[PR])
    for b in ARd + BKd + BHd + KHd + VDd:
        c.op("pool", lambda b=b: G_.memset(b[:], 0.0), writes=[b])
    for i in range(4):
        c.op("pool", lambda i=i: G_.memset(HF[i][:], 0.0), writes=[HF[i]])
        c.op("pool", lambda i=i: G_.memset(HB[i][:], 0.0), writes=[HB[i]])
    for i in range(NR):
        c.op("dve", lambda i=i: V.tensor_copy(out=X0[i][:, 1, :], in_=CST[:, 0:128]), reads=[CST], writes=[X0[i]])

    pT_v = pT.rearrange("(c p) t -> p c t", p=128)
    tmi = [0]

    def tm():
        t = TM[tmi[0] % NT_]
        tmi[0] += 1
        return t

    def halves(fn_engine, eng, out_buf, out_sel, make):
        for hb in range(2):
            ps = slice(hb * 64, hb * 64 + 64)
            cs = slice(hb * 64, hb * 64 + 64)
            make(ps, cs)

    unit = [0]
    ntile = T // TT
    for j in range(ntile):
        ts = slice(j * TT, (j + 1) * TT)
        if tile_hook is not None:
            tile_hook(j)
        c.dma_in("sp", PR, PR[:, :, 1:513], pT_v[:, :, ts])
        c.op("dve", lambda: V.tensor_copy(out=CAR[:], in_=PR[:, :, 512]), reads=[PR], writes=[CAR])
        for ct in range(16):
            d = tm()
            c.op("dve", lambda ct=ct, d=d: V.tensor_tensor(out=d[:], in0=PR[:, ct, 0:512], in1=PR[:, ct, 1:513], op=ALU.subtract),
                 reads=[PR], writes=[d])
            c.op("dve", lambda ct=ct, d=d: V.scalar_tensor_tensor(out=PR[:, ct, 1:513], in0=d[:], scalar=PAR[:, ct:ct + 1],
                                                                    in1=PR[:, ct, 1:513], op0=ALU.mult, op1=ALU.add),
                 reads=[PR, d, PAR], writes=[PR])
        c.op("act", lambda: A_.activation(out=TW[:], in_=PR[:, 12, 1:513], func=AF.Tanh), reads=[PR], writes=[TW])
        c.op("act", lambda: A_.activation(out=ACb[:], in_=PR[:, 13, 1:513], func=AF.Copy), reads=[PR], writes=[ACb])
        c.op("act", lambda: A_.activation(out=SG[:], in_=PR[:, 14:16, 1:513], func=AF.Sigmoid), reads=[PR], writes=[SG])
        for q in range(4):
            s_ = q
            r_ = PR[:, q, 1:513]; k_ = PR[:, 4 + q, 1:513]; v_ = PR[:, 8 + q, 1:513]
            c.op("pe", lambda: PE.matmul(PA[:], LORA[:, q * 128:(q + 1) * 128], TW[:], start=True, stop=True),
                 reads=[LORA, TW], writes=[PA])
            SIG = tm()
            c.op("act", lambda: A_.activation(out=SIG[:], in_=PA[:], func=AF.Sigmoid, bias=PAR[:, 16 + q:17 + q]),
                 reads=[PA, PAR], writes=[SIG])
            c.op("pe", lambda: PE.matmul(PB[:], LORA[:, 512 + q * 128:512 + (q + 1) * 128], ACb[:], start=True, stop=True),
                 reads=[LORA, ACb], writes=[PB])
            AA = tm()
            c.op("act", lambda: A_.activation(out=AA[:], in_=PB[:], func=AF.Sigmoid, bias=PAR[:, 20 + q:21 + q]),
                 reads=[PB, PAR], writes=[AA])

            def mmg():
                PE.matmul(PU[:], LORA[:, 1024 + q * 128:1024 + (q + 1) * 128], SG[:, 0, :], start=True, stop=False)
                return PE.matmul(PU[:], LORA[:, 1536 + q * 128:1536 + (q + 1) * 128], SG[:, 1, :], start=False, stop=True)
            c.op("pe", mmg, reads=[LORA, SG], writes=[PU])
            c.op("act", lambda: A_.activation(out=GATE[s_][:], in_=PU[:], func=AF.Copy), reads=[PU], writes=[GATE[s_]])
            CS = tm()
            c.op("dve", lambda: V.tensor_tensor_scan(out=CS[:], data0=RESET, data1=SIG[:], initial=0.0, op0=ALU.mult, op1=ALU.add),
                 reads=[CST, SIG], writes=[CS])
            KK = tm()
            c.op("dve", lambda: V.tensor_scalar(out=KK[:], in0=k_, scalar1=PAR[:, 24 + q:25 + q], scalar2=None, op0=ALU.mult),
                 reads=[PR, PAR], writes=[KK])
            SQ = TB[0]
            c.op("act", lambda: A_.activation(out=SQ[:], in_=KK[:], func=AF.Square), reads=[KK], writes=[SQ])
            c.op("pe", lambda: PE.matmul(PA[:], ONESB[:], SQ[:], start=True, stop=True), reads=[ONESB, SQ], writes=[PA])
            RN = tm()
            c.op("act", lambda: A_.activation(out=RN[:], in_=PA[:], func=AF.Sqrt), reads=[PA], writes=[RN])
            c.op("dve", lambda: V.tensor_scalar(out=RN[:], in0=RN[:], scalar1=1e-12, scalar2=None, op0=ALU.max), reads=[RN], writes=[RN])
            c.op("dve", lambda: V.reciprocal(out=RN[:], in_=RN[:]), reads=[RN], writes=[RN])
            KKN = tm()
            c.op("dve", lambda: V.tensor_tensor(out=KKN[:], in0=KK[:], in1=RN[:], op=ALU.mult), reads=[KK, RN], writes=[KKN])
            T1 = tm()
            c.op("dve", lambda: V.tensor_scalar(out=T1[:], in0=AA[:], scalar1=PAR[:, 28 + q:29 + q], scalar2=OMKA[:, q:q + 1],
                                               op0=ALU.mult, op1=ALU.add), reads=[AA, PAR, OMKA], writes=[T1])
            KM = tm()
            c.op("dve", lambda: V.tensor_tensor(out=KM[:], in0=k_, in1=T1[:], op=ALU.mult), reads=[PR, T1], writes=[KM])
            BV = tm()
            c.op("dve", lambda: V.tensor_tensor(out=BV[:], in0=KKN[:], in1=AA[:], op=ALU.mult), reads=[KKN, AA], writes=[BV])
            E1 = tm(); E2 = tm(); E3 = tm(); E4 = tm()
            c.op("act", lambda: A_.activation(out=E1[:], in_=CS[:], func=AF.Exp, scale=-LW_SCALE), reads=[CS], writes=[E1])
            c.op("act", lambda: A_.activation(out=E2[:], in_=CS[:], func=AF.Exp, scale=LW_SCALE), reads=[CS], writes=[E2])
            c.op("dve", lambda: V.tensor_tensor(out=E3[:], in0=CS[:], in1=SIG[:], op=ALU.subtract), reads=[CS, SIG], writes=[E3])
            c.op("act", lambda: A_.activation(out=E3[:], in_=E3[:], func=AF.Exp, scale=-LW_SCALE), reads=[E3], writes=[E3])
            CS3 = CS[:].rearrange("p (c t) -> p c t", t=64)
            c.op("dve", lambda: V.tensor_tensor(out=E4[:].rearrange("p (c t) -> p c t", t=64),
                                                  in0=CS3[:, :, 63:64].to_broadcast([128, 8, 64]), in1=CS3, op=ALU.subtract),
                 reads=[CS], writes=[E4])
            c.op("act", lambda: A_.activation(out=E4[:], in_=E4[:], func=AF.Exp, scale=-LW_SCALE), reads=[E4], writes=[E4])
            c.op("act", lambda: A_.activation(out=GAM[s_][:], in_=CS3[:, :, 63], func=AF.Exp, scale=-LW_SCALE),
                 reads=[CS], writes=[GAM[s_]])
            def bd(eng, outbuf, sel, in0, in1, neg=False):
                for hb in range(2):
                    ps = slice(hb * 64, hb * 64 + 64)
                    if sel is None:
                        o = outbuf[ps, :, hb * 64:hb * 64 + 64]
                    else:
                        o = outbuf[ps, :, sel, hb * 64:hb * 64 + 64]
                    a0 = in0[ps].rearrange("p (c t) -> p c t", t=64)
                    if in1 is None:
                        c.op("act", lambda: A_.activation(out=o, in_=a0, func=AF.Copy), reads=[PR], writes=[outbuf])
                        continue
                    a1 = in1[ps].rearrange("p (c t) -> p c t", t=64)
                    if neg:
                        c.op("dve", lambda: V.scalar_tensor_tensor(out=o, in0=a0, scalar=-1.0, in1=a1, op0=ALU.mult, op1=ALU.mult),
                             reads=[PR] + bd_reads, writes=[outbuf])
                    elif eng == "pool":
                        c.op("dve", lambda: V.tensor_tensor(out=o, in0=a0, in1=a1, op=ALU.mult), reads=[PR] + bd_reads, writes=[outbuf])
                    else:
                        c.op("dve", lambda: V.tensor_tensor(out=o, in0=a0, in1=a1, op=ALU.mult), reads=[PR] + bd_reads, writes=[outbuf])
            bd_reads = [KKN, E3]; bd("dve", ARd[s_], 0, KKN[:], E3[:], neg=True)
            bd_reads = [E1]; bd("pool", ARd[s_], 1, r_, E1[:])
            bd_reads = [BV, E2]; bd("dve", BKd[s_], 0, BV[:], E2[:])
            bd_reads = [KM, E2]; bd("pool", BKd[s_], 1, KM[:], E2[:])
            bd_reads = [BV, E4]; bd("dve", BHd[s_], None, BV[:], E4[:])
            bd_reads = [KM, E4]; bd("pool", KHd[s_], None, KM[:], E4[:])
            bd_reads = []; bd("pool", VDd[s_], None, v_, None)
            T2 = tm()
            c.op("dve", lambda: V.tensor_tensor(out=T2[:], in0=r_, in1=KM[:], op=ALU.mult), reads=[PR, KM], writes=[T2])
            T2b = TB[1]
            c.op("dve", lambda: V.tensor_scalar(out=T2b[:], in0=T2[:], scalar1=PAR[:, 32 + q:33 + q], scalar2=None, op0=ALU.mult),
                 reads=[T2, PAR], writes=[T2b])
            c.op("pe", lambda: PE.matmul(PB[:], ONESB[:], T2b[:], start=True, stop=True), reads=[ONESB, T2b], writes=[PB])
            c.op("dve", lambda: V.tensor_tensor(out=BON[s_][:], in0=PB[:], in1=v_, op=ALU.mult), reads=[PB, PR], writes=[BON[s_]])


        def unit_gen(q):
            s_ = q
            for ch in range(NCH):
                u = q
                pi = PI[q]
                Ad = ARd[s_][:, ch, 0, :]; Rd = ARd[s_][:, ch, 1, :]
                Bd = BKd[s_][:, ch, 0, :]; Kd = BKd[s_][:, ch, 1, :]
                AR2 = ARd[s_][:, ch, :, :].rearrange("p a b -> p (a b)")
                BK2 = BKd[s_][:, ch, :, :].rearrange("p a b -> p (a b)")

                def tr():
                    PE.transpose(PT[:, 0, :], Ad, IDB[:])
                    PE.transpose(PT[:, 1, :], BHd[s_][:, ch, :], IDB[:])
                    PE.transpose(PT[:, 2, :], KHd[s_][:, ch, :], IDB[:])
                    return PE.transpose(PT[:, 3, :], VDd[s_][:, ch, :], IDB[:])
                c.op("pe", tr, reads=[ARd[s_], BHd[s_], KHd[s_], VDd[s_], IDB], writes=[PT])
                c.op("act", lambda: A_.activation(out=TOK[u][:], in_=PT[:], func=AF.Copy), reads=[PT], writes=[TOK[u]])
                yield

                def g1():
                    PE.matmul(PA[:, 0:256], Bd, AR2, start=True, stop=True)
                    PE.matmul(PA[:, 256:512], Ad, BK2, start=True, stop=True)
                    return PE.matmul(PB[:, 0:128], Kd, Rd, start=True, stop=True)
                c.op("pe", g1, reads=[ARd[s_], BKd[s_]], writes=[PA, PB])
                c.op("dve", lambda: V.tensor_tensor(out=X0[u][:, 0, :], in0=PA[:, 0:128], in1=MSU, op=ALU.mult),
                     reads=[PA, CST], writes=[X0[u]])
                c.op("dve", lambda: V.tensor_tensor(out=ARB[u][:], in0=PA[:, 128:256], in1=MIU, op=ALU.mult),
                     reads=[PA, CST], writes=[ARB[u]])
                c.op("dve", lambda: V.tensor_tensor(out=NA[u][:], in0=PA[:, 256:512].rearrange("p (a b) -> p a b", a=2),
                                                    in1=MSL.unsqueeze(1).to_broadcast([128, 2, 128]), op=ALU.mult),
                     reads=[PA, CST], writes=[NA[u]])
                c.op("dve", lambda: V.tensor_tensor(out=ARK[u][:], in0=PB[:, 0:128], in1=MIU, op=ALU.mult),
                     reads=[PB, CST], writes=[ARK[u]])
                yield
                MScur = X0[u]
                Ncur_buf = NA[u]
                Ncur = NA[u][:, 0, :]
                for lev in range(6):
                    last = lev == 5
                    MSn = XA[u] if lev % 2 == 0 else XB[u]
                    MS2 = MScur[:, 0:2, :].rearrange("p a b -> p (a b)")

                    def lv():
                        if not last:
                            PE.matmul(pi[:, 0:256], Ncur, MS2, start=True, stop=False)
                        else:
                            PE.matmul(pi[:, 128:256], Ncur, MScur[:, 1, :], start=True, stop=False)
                        i = PE.matmul(pi[:, 128:256], IDB[:], MScur[:, 1, :], start=False, stop=True)
                        if not last:
                            i = PE.matmul(pi[:, 256:384], MScur[:, 0, :], Ncur, start=True, stop=True)
                        return i
                    c.op("pe", lv, reads=[Ncur_buf, MScur, IDB], writes=[pi])
                    if not last:
                        c.op("act", lambda: A_.activation(out=MSn[:].rearrange("p a b -> p (a b)"), in_=pi[:, 0:384], func=AF.Copy),
                             reads=[pi], writes=[MSn])
                        yield
                        MScur = MSn
                        Ncur_buf = MSn
                        Ncur = MSn[:, 2, :]
                    else:
                        c.op("act", lambda: A_.activation(out=TTb[u][:], in_=pi[:, 128:256], func=AF.Copy), reads=[pi], writes=[TTb[u]])
                        yield

                def pw():
                    PE.matmul(PB[:, 128:256], TOK[u][:, 0, :], TTb[u][:], start=True, stop=True)
                    return PE.matmul(PB[:, 256:384], NA[u][:, 1, :], TTb[u][:], start=True, stop=True)
                c.op("pe", pw, reads=[TOK[u], NA[u], TTb[u]], writes=[PB])
                c.op("act", lambda: A_.activation(out=PW[u][:].rearrange("p a b -> p (a b)"), in_=PB[:, 128:384], func=AF.Copy),
                     reads=[PB], writes=[PW[u]])
                yield

                def um():
                    PE.matmul(PU[:, 0:128], PW[u][:, 1, :], TOK[u][:, 3, :], start=True, stop=False)
                    return PE.matmul(PU[:, 0:128], PW[u][:, 0, :], HB[q][:], start=False, stop=True)
                c.op("pe", um, reads=[PW[u], TOK[u], HB[q]], writes=[PU])
                c.op("act", lambda: A_.activation(out=UB[u][:], in_=PU[:, 0:128], func=AF.Copy), reads=[PU], writes=[UB[u]])
                yield

                def hy():
                    PE.matmul(PU[:, 128:256], TOK[u][:, 2, :], TOK[u][:, 3, :], start=True, stop=False)
                    PE.matmul(PU[:, 128:256], TOK[u][:, 1, :], UB[u][:], start=False, stop=True)
                    PE.matmul(PU[:, 256:384], HB[q][:], Rd, start=True, stop=False)
                    PE.matmul(PU[:, 256:384], TOK[u][:, 3, :], ARK[u][:], start=False, stop=False)
                    return PE.matmul(PU[:, 256:384], UB[u][:], ARB[u][:], start=False, stop=True)
                c.op("pe", hy, reads=[TOK[u], UB[u], HB[q], ARd[s_], ARK[u], ARB[u]], writes=[PU])
                c.op("dve", lambda: V.scalar_tensor_tensor(out=HF[q][:], in0=HF[q][:], scalar=GAM[s_][:, ch:ch + 1], in1=PU[:, 128:256],
                                                           op0=ALU.mult, op1=ALU.add), reads=[HF[q], GAM[s_], PU], writes=[HF[q]])
                c.op("dve", lambda: V.tensor_scalar(out=YFq[q][0:64, ch * 64:(ch + 1) * 64], in0=PU[0:64, 256:320], scalar1=1.0, scalar2=None, op0=ALU.mult),
                     reads=[PU], writes=[YFq[q]])
                c.op("dve", lambda: V.tensor_scalar(out=YFq[q][64:128, ch * 64:(ch + 1) * 64], in0=PU[64:128, 320:384], scalar1=1.0, scalar2=None, op0=ALU.mult),
                     reads=[PU], writes=[YFq[q]])
                c.op("act", lambda: A_.activation(out=HB[q][:], in_=HF[q][:], func=AF.Copy), reads=[HF[q]], writes=[HB[q]])
                yield


        def epilogue(q):
            s_ = q
            YF = YFq[q]
            YB = TB[2]; YS = TB[0]
            c.op("act", lambda: A_.activation(out=YB[:], in_=YF[:], func=AF.Copy), reads=[YF], writes=[YB])
            c.op("act", lambda: A_.activation(out=YS[:], in_=YF[:], func=AF.Square), reads=[YF], writes=[YS])
            c.op("pe", lambda: PE.matmul(PA[:], ONESB[:], YB[:], start=True, stop=True), reads=[ONESB, YB], writes=[PA])
            c.op("pe", lambda: PE.matmul(PB[:], ONESB[:], YS[:], start=True, stop=True), reads=[ONESB, YS], writes=[PB])
            MEAN = tm(); M2 = tm(); VAR = tm()
            c.op("act", lambda: A_.activation(out=MEAN[:], in_=PA[:], func=AF.Copy, scale=1.0 / 64), reads=[PA], writes=[MEAN])
            c.op("dve", lambda: V.tensor_tensor(out=M2[:], in0=MEAN[:], in1=MEAN[:], op=ALU.mult), reads=[MEAN], writes=[M2])
            c.op("dve", lambda: V.scalar_tensor_tensor(out=VAR[:], in0=PB[:], scalar=1.0 / 64, in1=M2[:], op0=ALU.mult, op1=ALU.subtract),
                 reads=[PB, M2], writes=[VAR])
            c.op("act", lambda: A_.activation(out=VAR[:], in_=VAR[:], func=AF.Sqrt, bias=EPSL[:]), reads=[VAR, EPSL], writes=[VAR])
            c.op("dve", lambda: V.reciprocal(out=VAR[:], in_=VAR[:]), reads=[VAR], writes=[VAR])
            c.op("dve", lambda: V.tensor_tensor(out=YF[:], in0=YF[:], in1=MEAN[:], op=ALU.subtract), reads=[YF, MEAN], writes=[YF])
            c.op("dve", lambda: V.tensor_tensor(out=YF[:], in0=YF[:], in1=VAR[:], op=ALU.mult), reads=[YF, VAR], writes=[YF])
            c.op("dve", lambda: V.tensor_scalar(out=YF[:], in0=YF[:], scalar1=PAR[:, 36 + q:37 + q], scalar2=PAR[:, 40 + q:41 + q],
                                               op0=ALU.mult, op1=ALU.add), reads=[YF, PAR], writes=[YF])
            c.op("dve", lambda: V.tensor_tensor(out=YF[:], in0=YF[:], in1=BON[s_][:], op=ALU.add), reads=[YF, BON[s_]], writes=[YF])
            ob = OUTB[q % 2]
            c.op("dve", lambda: V.tensor_tensor(out=ob[:], in0=YF[:], in1=GATE[s_][:], op=ALU.mult), reads=[YF, GATE[s_]], writes=[ob])
            c.dma_out("sp", ob, yT[q * 128:(q + 1) * 128, ts], ob[:])


        import os as _os
        STAG = int(_os.environ.get("RW_STAG", "3"))
        gens = [unit_gen(q) for q in range(4)]
        live = []
        step = 0
        started = 0
        while live or started < 4:
            if started < 4 and step >= started * STAG:
                live.append(gens[started])
                started += 1
            nxt = []
            for gq in live:
                try:
                    next(gq)
                    nxt.append(gq)
                except StopIteration:
                    pass
            live = nxt
            step += 1
        for q in range(4):
            epilogue(q)
        c.op("dve", lambda: V.tensor_copy(out=PR[:, :, 0], in_=CAR[:]), reads=[CAR], writes=[PR])
    print(f"[rwkv] ops={c.nops} waits={c.nwaits}")


def ssd_consts():
    f32 = np.float32
    p = np.arange(128)[:, None]
    j = np.arange(128)[None, :]
    idf = (p == j).astype(f32)
    valid = ((p // 64) == (j // 64)) & (j >= p)
    neg2 = np.where(valid, 0.0, -1e5).astype(f32)
    ones = np.ones((128, 128), f32)
    reset = np.ones((128, 512), f32)
    reset[:, ::64] = 0.0
    cst = np.concatenate([idf, neg2, ones, reset], axis=1)
    selp = np.zeros((8, 4, 128), f32)
    for h in range(8):
        selp[h, h // 2, (h % 2) * 64:(h % 2) * 64 + 64] = 1.0
    selh = np.zeros((8, 8, 128), f32)
    for h in range(8):
        selh[h, h, :] = 1.0
    sel = np.concatenate([selp.reshape(8, 512), selh.reshape(8, 1024)], axis=1)
    return np.ascontiguousarray(cst), np.ascontiguousarray(sel)


def emit_ssd(nc, c, io, T, tile_hook=None):
    pT = io["pT"]; par = io["par"]; hpar = io["hpar"]; cst = io["cst"]; sel = io["sel"]; yT = io["yT"]
    V, A_, G_, PE = nc.vector, nc.scalar, nc.gpsimd, nc.tensor

    PAR = c.sbuf("PAR", [128, 82], F32)
    HPAR = c.sbuf("HPAR", [8, 2], F32)
    ANEG = c.sbuf("ANEG", [8, 1], F32)
    CST = c.sbuf("CST", [128, 896], F32)
    SEL = c.sbuf("SEL", [8, 1536], F32)
    IDB = c.sbuf("IDB", [128, 128], BF16)
    ONESF = c.sbuf("ONESF", [128, 128], BF16)
    EPSB = c.sbuf("EPSB", [128, 1], F32)
    Z = c.sbuf("Z", [128, 4, 512], F32)
    XSP = c.sbuf("XSP", [128, 6, 515], F32)
    DTR = c.sbuf("DTR", [8, 512], F32)
    DT = c.sbuf("DT", [8, 512], F32)
    DA = c.sbuf("DA", [8, 512], F32)
    ACS = c.sbuf("ACS", [8, 512], F32)
    XSA = c.sbuf("XSA", [128, 4, 512], F32)
    XC = c.sbuf("XC", [128, 4, 512], F32)
    CA = c.sbuf("CA", [128, 512], F32)
    BAb = c.sbuf("BAb", [128, 512], BF16)
    CAb = c.sbuf("CAb", [128, 512], BF16)
    ACSB = c.sbuf("ACSB", [128, 8, 512], F32)
    CDS = c.sbuf("CDS", [128, 8, 8], F32)
    CE = c.sbuf("CE", [128, 8, 512], BF16)
    XCT = c.sbuf("XCT", [128, 4, 512], BF16)
    XDT = c.sbuf("XDT", [128, 4, 512], BF16)
    BTK = c.sbuf("BTK", [128, 4, 128], BF16)
    ACST = c.sbuf("ACST", [128, 4, 8], F32)
    DEC = c.sbuf("DEC", [128, 4, 8], F32)
    SEG = [c.sbuf(f"SEG{i}", [128, 8, 128], F32) for i in range(2)]
    MTL = [c.sbuf(f"MTL{i}", [128, 8, 128], BF16) for i in range(2)]
    SBST = c.sbuf("SBST", [128, 8, 512], BF16)
    SF = c.sbuf("SF", [128, 512], F32)
    Y2 = c.sbuf("Y2", [128, 4, 512], F32)
    YSQ = c.sbuf("YSQ", [128, 4, 512], BF16)
    RST = c.sbuf("RST", [128, 512], F32)
    TM = [c.sbuf(f"TM{i}", [128, 512], F32) for i in range(4)]
    OUTB = [c.sbuf(f"OUTB{i}", [128, 512], BF16) for i in range(2)]
    PS = [c.psum(f"PS{i}", [128, 512], F32) for i in range(3)]
    PYQ = [c.psum(f"PYQ{i}", [128, 512], F32) for i in range(4)]
    PTB = c.psum("PTB", [128, 4, 128], BF16)
    psi = [0]

    def nps():
        p = PS[psi[0] % 3]
        psi[0] += 1
        return p
    tmi = [0]

    def tm():
        t = TM[tmi[0] % 4]
        tmi[0] += 1
        return t

    c.dma_in("sp", PAR, PAR[:], par[:, :])
    c.dma_in("sp", HPAR, HPAR[:], hpar[:, :])
    c.dma_in("sp", CST, CST[:], cst[:, :])
    c.dma_in("sp", SEL, SEL[:], sel[:, :])
    IDF = CST[:, 0:128]; NEG2 = CST[:, 128:256]; RESET = CST[:, 384:896]
    c.op("dve", lambda: V.tensor_copy(out=IDB[:], in_=CST[:, 0:128]), reads=[CST], writes=[IDB])
    c.op("dve", lambda: V.tensor_copy(out=ONESF[:], in_=CST[:, 256:384]), reads=[CST], writes=[ONESF])
    c.op("dve", lambda: V.memset(EPSB[:], 1e-6), writes=[EPSB])
    c.op("act", lambda: A_.activation(out=ANEG[:], in_=HPAR[:, 1:2], func=AF.Exp), reads=[HPAR], writes=[ANEG])
    c.op("dve", lambda: V.tensor_scalar(out=ANEG[:], in0=ANEG[:], scalar1=-1.0, scalar2=None, op0=ALU.mult), reads=[ANEG], writes=[ANEG])
    c.op("pool", lambda: G_.memset(XSP[:], 0.0), writes=[XSP])
    c.op("pool", lambda: G_.memset(SF[:], 0.0), writes=[SF])

    pT_v = pT.rearrange("(c p) t -> p c t", p=128)
    CW = lambda j, i: PAR[:, 44 + 6 * j + i:45 + 6 * j + i]
    ntile = T // TT
    for jt in range(ntile):
        ts = slice(jt * TT, (jt + 1) * TT)
        if tile_hook is not None:
            tile_hook(jt)
        c.dma_in("sp", Z, Z[:], pT_v[:, 0:4, ts])
        c.dma_in("sp", XSP, XSP[:, :, 3:515], pT_v[:, 4:10, ts])
        c.dma_in("sp", DTR, DTR[:], pT[10 * 128:10 * 128 + 8, ts])
        for i in range(6):
            acc = tm()
            c.op("dve", lambda: V.tensor_scalar(out=acc[:], in0=XSP[:, i, 0:512], scalar1=CW(0, i), scalar2=None, op0=ALU.mult),
                 reads=[XSP, PAR], writes=[acc])
            for j in range(1, 4):
                c.op("dve", lambda: V.scalar_tensor_tensor(out=acc[:], in0=XSP[:, i, j:j + 512], scalar=CW(j, i), in1=acc[:],
                                                           op0=ALU.mult, op1=ALU.add), reads=[XSP, PAR, acc], writes=[acc])
            if i < 4:
                c.op("act", lambda: A_.activation(out=XSA[:, i, :], in_=acc[:], func=AF.Silu, bias=PAR[:, 68 + i:69 + i]),
                     reads=[acc, PAR], writes=[XSA])
            elif i == 4:
                c.op("act", lambda: A_.activation(out=BAb[:], in_=acc[:], func=AF.Silu, bias=PAR[:, 68 + i:69 + i]),
                     reads=[acc, PAR], writes=[BAb])
            else:
                c.op("act", lambda: A_.activation(out=CA[:], in_=acc[:], func=AF.Silu, bias=PAR[:, 68 + i:69 + i]),
                     reads=[acc, PAR], writes=[CA])
                c.op("act", lambda: A_.activation(out=CAb[:], in_=CA[:], func=AF.Copy), reads=[CA], writes=[CAb])
        c.op("dve", lambda: V.tensor_copy(out=XSP[:, :, 0:3], in_=XSP[:, :, 512:515]), reads=[XSP], writes=[XSP])
        c.op("act", lambda: A_.activation(out=DT[:], in_=DTR[:], func=AF.Exp, bias=HPAR[:, 0:1]), reads=[DTR, HPAR], writes=[DT])
        c.op("dve", lambda: V.tensor_scalar(out=DT[:], in0=DT[:], scalar1=1.0, scalar2=None, op0=ALU.add), reads=[DT], writes=[DT])
        c.op("act", lambda: A_.activation(out=DT[:], in_=DT[:], func=AF.Ln), reads=[DT], writes=[DT])
        c.op("dve", lambda: V.tensor_scalar(out=DA[:], in0=DT[:], scalar1=ANEG[:, 0:1], scalar2=None, op0=ALU.mult),
             reads=[DT, ANEG], writes=[DA])
        c.op("dve", lambda: V.tensor_tensor_scan(out=ACS[:], data0=RESET[0:8, :], data1=DA[:], initial=0.0, op0=ALU.mult, op1=ALU.add),
             reads=[CST, DA], writes=[ACS])
        for h in range(8):
            p = nps()
            c.op("pe", lambda: PE.matmul(p[:], SEL[:, 512 + h * 128:512 + (h + 1) * 128], ACS[:], start=True, stop=True),
                 reads=[SEL, ACS], writes=[p])
            c.op("act", lambda: A_.activation(out=ACSB[:, h, :], in_=p[:], func=AF.Copy), reads=[p], writes=[ACSB])
        c.op("act", lambda: A_.activation(out=CDS[:], in_=ACSB[:].rearrange("p h (c t) -> p h c t", t=64)[:, :, :, 63], func=AF.Exp),
             reads=[ACSB], writes=[CDS])
        for q in range(4):
            p = nps()
            c.op("pe", lambda: PE.matmul(p[:], SEL[:, q * 128:(q + 1) * 128], DT[:], start=True, stop=True),
                 reads=[SEL, DT], writes=[p])
            c.op("dve", lambda: V.tensor_tensor(out=XC[:, q, :], in0=XSA[:, q, :], in1=p[:], op=ALU.mult), reads=[XSA, p], writes=[XC])
        for h in range(8):
            t = tm()
            c.op("act", lambda: A_.activation(out=t[:], in_=ACSB[:, h, :], func=AF.Exp), reads=[ACSB], writes=[t])
            c.op("dve", lambda: V.tensor_tensor(out=CE[:, h, :], in0=CA[:], in1=t[:], op=ALU.mult), reads=[CA, t], writes=[CE])
        for blk in range(4):
            bs = slice(blk * 128, (blk + 1) * 128)
            p = nps()
            c.op("pe", lambda: PE.transpose(p[:, 0:8], ACS[0:8, bs], CST[0:8, 0:8]), reads=[ACS, CST], writes=[p])
            c.op("act", lambda: A_.activation(out=ACST[:, blk, :], in_=p[:, 0:8], func=AF.Copy), reads=[p], writes=[ACST])
            for a in range(2):
                ps_ = slice(a * 64, a * 64 + 64)
                ce = blk * 128 + a * 64 + 63
                c.op("dve", lambda: V.tensor_tensor(out=DEC[ps_, blk, :], in0=ACSB[ps_, :, ce], in1=ACST[ps_, blk, :], op=ALU.subtract),
                     reads=[ACSB, ACST], writes=[DEC])
            c.op("act", lambda: A_.activation(out=DEC[:, blk, :], in_=DEC[:, blk, :], func=AF.Exp), reads=[DEC], writes=[DEC])
            c.op("pe", lambda: PE.transpose(PTB[:, 0, :], BAb[:, bs], IDB[:]), reads=[BAb, IDB], writes=[PTB])
            import os
            bv = os.environ.get("BVAR", "act")
            if bv == "act":
                c.op("act", lambda: A_.activation(out=BTK[:, blk, :], in_=PTB[:, 0, :], func=AF.Copy), reads=[PTB], writes=[BTK])
            elif bv == "dve":
                c.op("dve", lambda: V.tensor_scalar(out=BTK[:, blk, :], in0=PTB[:, 0, :], scalar1=1.0, scalar2=None, op0=ALU.mult), reads=[PTB], writes=[BTK])
            elif bv == "none":
                pass
            p = nps()

            def trx():
                for q in range(4):
                    i = PE.transpose(p[:, q * 128:(q + 1) * 128], XC[:, q, bs], IDF)
                return i
            c.op("pe", trx, reads=[XC, CST], writes=[p])
            c.op("act", lambda: A_.activation(out=XCT[:, blk, :], in_=p[:], func=AF.Copy), reads=[p], writes=[XCT])
            c.op("dve", lambda: V.tensor_tensor(out=XDT[:, blk, :].rearrange("p (h e) -> p h e", e=64),
                                                in0=p[:].rearrange("p (h e) -> p h e", e=64),
                                                in1=DEC[:, blk, :].unsqueeze(2).to_broadcast([128, 8, 64]), op=ALU.mult),
                 reads=[p, DEC], writes=[XDT])
        for g in range(8):
            blk, a = g // 2, g % 2
            ps_ = slice(a * 64, a * 64 + 64)
            c.op("act", lambda: A_.activation(out=SBST[:, g, :], in_=SF[:], func=AF.Copy), reads=[SF], writes=[SBST])
            p = nps()
            c.op("pe", lambda: PE.matmul(p[:], BTK[ps_, blk, :], XDT[ps_, blk, :], start=True, stop=True), reads=[BTK, XDT], writes=[p])
            c.op("dve", lambda: V.tensor_tensor(out=SF[:].rearrange("p (h e) -> p h e", e=64), in0=SF[:].rearrange("p (h e) -> p h e", e=64),
                                                in1=CDS[:, :, g].unsqueeze(2).to_broadcast([128, 8, 64]), op=ALU.mult),
                 reads=[SF, CDS], writes=[SF])
            c.op("dve", lambda: V.tensor_tensor(out=SF[:], in0=SF[:], in1=p[:], op=ALU.add), reads=[SF, p], writes=[SF])
        PYq = PYQ
        for blk in range(4):
            bs = slice(blk * 128, (blk + 1) * 128)
            sg = SEG[blk % 2]; ml = MTL[blk % 2]
            p = nps()
            c.op("pe", lambda: PE.matmul(p[:, 0:128], BAb[:, bs], CAb[:, bs], start=True, stop=True), reads=[BAb, CAb], writes=[p])
            c.op("dve", lambda: V.tensor_tensor(out=sg[:], in0=ACSB[:, :, bs], in1=ACST[:, blk, :].unsqueeze(2).to_broadcast([128, 8, 128]),
                                                op=ALU.subtract), reads=[ACSB, ACST], writes=[sg])
            c.op("dve", lambda: V.tensor_tensor(out=sg[:], in0=sg[:], in1=NEG2.unsqueeze(1).to_broadcast([128, 8, 128]), op=ALU.add),
                 reads=[sg, CST], writes=[sg])
            c.op("act", lambda: A_.activation(out=sg[:], in_=sg[:], func=AF.Exp), reads=[sg], writes=[sg])
            c.op("dve", lambda: V.tensor_tensor(out=ml[:], in0=sg[:], in1=p[:, 0:128].unsqueeze(1).to_broadcast([128, 8, 128]), op=ALU.mult),
                 reads=[sg, p], writes=[ml])
            for q in range(4):
                def ym():
                    i = None
                    for h2 in range(2):
                        h = 2 * q + h2
                        o = PYq[q][h2 * 64:h2 * 64 + 64, bs]
                        PE.matmul(o, XCT[:, blk, h * 64:(h + 1) * 64], ml[:, h, :], start=True, stop=False)
                        PE.matmul(PYq[q][h2 * 64:h2 * 64 + 64, blk * 128:blk * 128 + 64], SBST[:, 2 * blk, h * 64:(h + 1) * 64],
                                  CE[:, h, blk * 128:blk * 128 + 64], start=False, stop=False)
                        i = PE.matmul(PYq[q][h2 * 64:h2 * 64 + 64, blk * 128 + 64:blk * 128 + 128], SBST[:, 2 * blk + 1, h * 64:(h + 1) * 64],
                                      CE[:, h, blk * 128 + 64:blk * 128 + 128], start=False, stop=True)
                    return i
                c.op("pe", ym, reads=[XCT, ml, SBST, CE], writes=[PYq[q]])
        for q in range(4):
            t = tm()
            c.op("dve", lambda: V.scalar_tensor_tensor(out=Y2[:, q, :], in0=XSA[:, q, :], scalar=PAR[:, 74 + q:75 + q], in1=PYq[q][:],
                                                       op0=ALU.mult, op1=ALU.add), reads=[XSA, PAR, PYq[q]], writes=[Y2])
            c.op("act", lambda: A_.activation(out=t[:], in_=Z[:, q, :], func=AF.Silu), reads=[Z], writes=[t])
            c.op("dve", lambda: V.tensor_tensor(out=Y2[:, q, :], in0=Y2[:, q, :], in1=t[:], op=ALU.mult), reads=[Y2, t], writes=[Y2])
            c.op("act", lambda: A_.activation(out=YSQ[:, q, :], in_=Y2[:, q, :], func=AF.Square), reads=[Y2], writes=[YSQ])
        p = nps()

        def mmn():
            for q in range(4):
                i = PE.matmul(p[:], ONESF[:], YSQ[:, q, :], start=(q == 0), stop=(q == 3))
            return i
        c.op("pe", mmn, reads=[ONESF, YSQ], writes=[p])
        c.op("act", lambda: A_.activation(out=RST[:], in_=p[:], func=AF.Sqrt, scale=1.0 / 512, bias=EPSB[:]), reads=[p, EPSB], writes=[RST])
        c.op("dve", lambda: V.reciprocal(out=RST[:], in_=RST[:]), reads=[RST], writes=[RST])
        for q in range(4):
            ob = OUTB[q % 2]
            c.op("dve", lambda: V.scalar_tensor_tensor(out=ob[:], in0=Y2[:, q, :], scalar=PAR[:, 78 + q:79 + q], in1=RST[:],
                                                       op0=ALU.mult, op1=ALU.mult), reads=[Y2, PAR, RST], writes=[ob])
            c.dma_out("sp", ob, yT[q * 128:(q + 1) * 128, ts], ob[:])
    print(f"[ssd] ops={c.nops} waits={c.nwaits}")


import numpy as np

NH = 16
HD = 128
NQ = 2048
HALO = 512
NK = NQ + HALO
WIN = 640
NEG = -30000.0


def attn_bias(rel_bias):
    qi = np.arange(128)[:, None]
    kj = np.arange(WIN)[None, :]
    rel = np.clip(512 + qi - kj, -63, 256) + 63
    a = qi // 64
    kc = kj // 64
    valid = (kc >= a) & (kc <= a + 8)
    out = rel_bias[:, rel]
    out = np.where(valid[None], out, np.float32(NEG)).astype(np.float32)
    return np.ascontiguousarray(out)


def halo_mask(th):
    m = np.zeros((128, 4, WIN), np.float32)
    if th == 0:
        for qt in range(4):
            m[:, qt, :HALO - 128 * qt] = NEG
    return np.ascontiguousarray(m.reshape(128, 4 * WIN))


def emit_attn(nc, c, io):
    qT = io["qT"]; kT = io["kT"]; vT = io["vT"]; bias = io["bias"]; hm = io["hm"]; idm = io["idm"]; oT = io["oT"]
    V, A_, G_, PE = nc.vector, nc.scalar, nc.gpsimd, nc.tensor
    scale = HD ** -0.5

    BIAS = c.sbuf("BIAS", [128, NH, WIN], F32)
    HM = c.sbuf("HM", [128, 4, WIN], F32)
    IDB = c.sbuf("IDB", [128, 128], BF16)
    QT = [c.sbuf(f"QT{i}", [128, NQ], BF16) for i in range(2)]
    KT = [c.sbuf(f"KT{i}", [128, NK], BF16) for i in range(2)]
    VT = [c.sbuf(f"VT{i}", [128, NK], BF16) for i in range(2)]
    VTOK = [c.sbuf(f"VTOK{i}", [128, NK // 128, 128], BF16) for i in range(2)]
    S = [c.sbuf(f"S{i}", [128, WIN], F32) for i in range(4)]
    PB = [c.sbuf(f"PB{i}", [128, WIN], BF16) for i in range(4)]
    PTS = [c.sbuf(f"PTS{i}", [128, 5, 128], BF16) for i in range(4)]
    MX = [c.sbuf(f"MX{i}", [128, 4], F32) for i in range(4)]
    OB = [c.sbuf(f"OB{i}", [128, 512], BF16) for i in range(2)]
    PS = [c.psum(f"PS{i}", [128, 1024], F32) for i in range(2)]
    PTP = [c.psum(f"PTP{i}", [128, 8, 128], BF16) for i in range(2)]
    PO = [c.psum(f"PO{i}", [128, 512], F32) for i in range(2)]

    c.dma_in("sp", BIAS, BIAS[:], bias.rearrange("h q k -> q h k"))
    c.dma_in("sp", HM, HM[:].rearrange("p a b -> p (a b)"), hm[:, :])
    c.dma_in("pool", IDB, IDB[:], idm[:, :])

    it = [0]
    for h in range(NH):
        hb = h % 2
        rows = slice(h * HD, (h + 1) * HD)
        c.dma_in("pool", QT[hb], QT[hb][:], qT[rows, :])
        c.dma_in("sp", KT[hb], KT[hb][:], kT[rows, :])
        c.dma_in("sp", VT[hb], VT[hb][:], vT[rows, :])
        for g in range(NK // 128 // 4):
            pt = PTP[g % 2]

            def tr():
                for b4 in range(4):
                    blk = g * 4 + b4
                    i = PE.transpose(pt[:, b4, :], VT[hb][:, blk * 128:(blk + 1) * 128], IDB[:])
                return i
            c.op("pe", tr, reads=[VT[hb], IDB], writes=[pt])
            c.op("act", lambda: A_.activation(out=VTOK[hb][:, g * 4:(g + 1) * 4, :], in_=pt[:, 0:4, :], func=AF.Copy),
                 reads=[pt], writes=[VTOK[hb]])
        def unit(qt):
            i4 = qt % 4
            ps = PS[qt % 2]
            s = S[i4]; mx = MX[i4]; pb = PB[i4]; pts = PTS[i4]

            def sm():
                PE.matmul(ps[:, 0:512], QT[hb][:, qt * 128:(qt + 1) * 128], KT[hb][:, qt * 128:qt * 128 + 512], start=True, stop=True)
                return PE.matmul(ps[:, 512:640], QT[hb][:, qt * 128:(qt + 1) * 128], KT[hb][:, qt * 128 + 512:qt * 128 + 640],
                                 start=True, stop=True)
            c.op("pe", sm, reads=[QT[hb], KT[hb]], writes=[ps])
            yield
            c.op("dve", lambda: V.scalar_tensor_tensor(out=s[:], in0=ps[:, 0:WIN], scalar=-scale, in1=BIAS[:, h, :],
                                                       op0=ALU.mult, op1=ALU.subtract), reads=[ps, BIAS], writes=[s])
            yield
            if qt < 4:
                c.op("dve", lambda: V.tensor_tensor(out=s[:], in0=s[:], in1=HM[:, qt, :], op=ALU.subtract), reads=[s, HM], writes=[s])
                yield
            c.op("dve", lambda: V.tensor_reduce(out=mx[:, 1:2], in_=s[:], op=ALU.min, axis=AX.X), reads=[s], writes=[mx])
            yield
            c.op("act", lambda: A_.activation(out=s[:], in_=s[:], func=AF.Exp, scale=-1.0, bias=mx[:, 1:2], accum_out=mx[:, 2:3]),
                 reads=[s, mx], writes=[s, mx])
            yield
            c.op("dve", lambda: V.reciprocal(out=mx[:, 3:4], in_=mx[:, 2:3]), reads=[mx], writes=[mx])
            yield
            c.op("dve", lambda: V.tensor_scalar(out=pb[:], in0=s[:], scalar1=mx[:, 3:4], scalar2=None, op0=ALU.mult),
                 reads=[s, mx], writes=[pb])
            yield
            pt = PTP[qt % 2]

            def tp():
                for b5 in range(5):
                    i = PE.transpose(pt[:, b5, :], pb[:, b5 * 128:(b5 + 1) * 128], IDB[:])
                return i
            c.op("pe", tp, reads=[pb, IDB], writes=[pt])
            yield
            c.op("act", lambda: A_.activation(out=pts[:], in_=pt[:, 0:5, :], func=AF.Copy), reads=[pt], writes=[pts])
            yield
            po = PO[(qt // 4) % 2]

            def pv():
                for b5 in range(5):
                    i = PE.matmul(po[:, (qt % 4) * 128:(qt % 4 + 1) * 128], VTOK[hb][:, qt + b5, :], pts[:, b5, :],
                                  start=(b5 == 0), stop=(b5 == 4))
                return i
            c.op("pe", pv, reads=[VTOK[hb], pts], writes=[po])
            yield

        nqt = NQ // 128
        for g4 in range(nqt // 4):
            pend = [unit(g4 * 4 + k) for k in range(4)]
            live = [pend.pop(0), pend.pop(0)]
            while live:
                nxt = []
                for gq in live:
                    try:
                        next(gq)
                        nxt.append(gq)
                    except StopIteration:
                        if pend:
                            nxt.append(pend.pop(0))
                live = nxt
            ob = OB[g4 % 2]
            po = PO[g4 % 2]
            c.op("act", lambda: A_.activation(out=ob[:], in_=po[:], func=AF.Copy), reads=[po], writes=[ob])
            c.dma_out("sp", ob, oT[rows, g4 * 512:(g4 + 1) * 512], ob[:])
    print(f"[attn] ops={c.nops} waits={c.nwaits}")


from concourse.bass_utils import run_bass_kernel_spmd

I32 = mybir.dt.int32
_CACHE = {}
T_SEQ = 4096
NTH = 2048


def build_fused():
    nc = bass.Bass("TRN2", target_bir_lowering=False)
    T = T_SEQ

    def ext(name, shape, dt=F32):
        return nc.dram_tensor(name, list(shape), dt, kind="ExternalInput").ap()
    xT = ext("xT", [D, T]); xTh = ext("xTh", [D, NTH]); cid = ext("cid", [1, 8], I32)
    g_a = ext("g_a", [128, 16]); g_b = ext("g_b", [128, 64]); g_c = ext("g_c", [128, 48])
    w_in = ext("w_in", [NCT, 128, D])
    par = ext("par", [128, 82]); hpar = ext("hpar", [8, 2]); lora = ext("lora", [128, 2048])
    rcst = ext("rcst", [128, 1152]); scst = ext("scst", [128, 896]); sel = ext("sel", [8, 1536])
    w_o0 = ext("w_o0", [DC, 128, D]); w_g0 = ext("w_g0", [FC, 128, D]); w_u0 = ext("w_u0", [FC, 128, D])
    w_d0 = ext("w_d0", [DC, 128, DFF]); w_qkv = ext("w_qkv", [48, 128, D])
    w_o1 = ext("w_o1", [DC, 128, D]); w_g1 = ext("w_g1", [FC, 128, D]); w_u1 = ext("w_u1", [FC, 128, D])
    w_d1 = ext("w_d1", [DC, 128, DFF])
    abias = ext("abias", [NH, 128, WIN]); hm = ext("hm", [128, 4 * WIN]); idm = ext("idm", [128, 128])
    xo = nc.dram_tensor("xo", [D, NTH], F32, kind="ExternalOutput").ap()
    po = nc.dram_tensor("po_i", [NCT * 128, T], F32).ap()
    yr_t = nc.dram_tensor("yr", [512, T], BF16)
    ys_t = nc.dram_tensor("ys", [512, T], BF16)
    yar_t = nc.dram_tensor("yall_r", [8 * 512, T], BF16)
    yas_t = nc.dram_tensor("yall_s", [8 * 512, T], BF16)
    ymix = nc.dram_tensor("ymix", [D, NTH], BF16).ap()
    x1 = nc.dram_tensor("x1_i", [D, NTH], F32).ap()
    qkv = nc.dram_tensor("qkv_i", [3 * D, NTH], F32).ap()
    kvh_t = nc.dram_tensor("kvh", [2 * D, HALO], BF16)
    kvall_t = nc.dram_tensor("kvall", [8 * 2 * D, HALO], BF16)
    kTi = nc.dram_tensor("kT_i", [D, NK], BF16).ap()
    vTi = nc.dram_tensor("vT_i", [D, NK], BF16).ap()
    oTi = nc.dram_tensor("oT_i", [D, NTH], BF16).ap()
    yr = yr_t.ap(); ys = ys_t.ap(); yar = yar_t.ap(); yas = yas_t.ap(); kvh = kvh_t.ap(); kvall = kvall_t.ap()

    c = Ctx(nc)
    g = nc.gpsimd
    cc_sem = c.stack.enter_context(nc.semaphore("cc_sem"))
    xs_sem = c.stack.enter_context(nc.semaphore("xs_sem"))
    CID = c.stack.enter_context(nc.sbuf_tensor("CIDS", [1, 8], I32))
    DUM = c.stack.enter_context(nc.sbuf_tensor("DUMMY", [1, 8], F32))
    regs = [c.stack.enter_context(g.register(f"cidr{i}")) for i in range(4)]
    xs_cnt = [0]
    cc_cnt = [0]

    def xdma(out, in_):
        g.dma_start(out=out, in_=in_).then_inc(xs_sem, 16)
        xs_cnt[0] += 16

    def xwait():
        g.wait_ge(xs_sem, xs_cnt[0])

    class _DumBuf:
        pass

    def sync_point():
        fake = _DumBuf(); fake.writes = {}; fake.reads = {}; fake.space = "sbuf"; fake.name = "dummy"
        c.op("pool", lambda: g.memset(DUM[:], 0.0), writes=[fake])
        c.barrier()

    c.prefix = "a_"
    xdma(CID[:, :], cid[:, :])
    emit_dense(nc, c, {"xT": xT, "gains": g_a, "w_p": w_in, "po": po}, T, False, False, NCT, False, name="inproj")
    c.end_phase("r_")
    emit_rwkv(nc, c, {"pT": po[0:16 * 128, :], "par": par, "lora": lora, "cst": rcst, "yT": yr}, T)
    c.end_phase("s_")
    xwait()
    g.collective_compute("AllGather", ALU.bypass, replica_groups=[list(range(8))],
                         ins=[yr_t.ap().opt()], outs=[yar_t.ap().opt()]).then_inc(cc_sem)
    cc_cnt[0] += 1
    emit_ssd(nc, c, {"pT": po[16 * 128:NCT * 128, :], "par": par, "hpar": hpar, "cst": scst, "sel": sel, "yT": ys}, T)
    c.end_phase("b_")
    g.wait_ge(cc_sem, cc_cnt[0])
    g.collective_compute("AllGather", ALU.bypass, replica_groups=[list(range(8))],
                         ins=[ys_t.ap().opt()], outs=[yas_t.ap().opt()]).then_inc(cc_sem)
    cc_cnt[0] += 1
    g.wait_ge(cc_sem, cc_cnt[0])
    for i in range(4):
        g.reg_load(regs[i], CID[0:1, i:i + 1])
    r0 = g.snap(regs[0]); r1 = g.snap(regs[1]); sv = g.snap(regs[2]); hr = g.snap(regs[3])
    yvr = yar.rearrange("(k r) (h t) -> k r h t", r=512, h=2)
    yvs = yas.rearrange("(k r) (h t) -> k r h t", r=512, h=2)
    for qi, (rv, yv) in enumerate([(r0, yvr), (r1, yvr), (r0, yvs), (r1, yvs)]):
        src = yv[bass.ds(rv, 1), :, bass.ds(sv, 1), :].rearrange("k r h t -> (k r) (h t)")
        xdma(ymix[qi * 512:(qi + 1) * 512, :], src)
    xwait()
    sync_point()
    emit_dense(nc, c, {"xT": xTh, "gains": g_b, "yT": ymix, "w_o": w_o0, "w_g": w_g0, "w_u": w_u0, "w_d": w_d0,
                       "w_p": w_qkv, "po": qkv, "xo": x1}, NTH, True, True, 48, True, name="block0")
    c.end_phase("c_")
    xdma(kvh[0:D, :], qkv[D:2 * D, NTH - HALO:NTH])
    xdma(kvh[D:2 * D, :], qkv[2 * D:3 * D, NTH - HALO:NTH])
    xdma(kTi[:, HALO:NK], qkv[D:2 * D, :])
    xdma(vTi[:, HALO:NK], qkv[2 * D:3 * D, :])
    xwait()
    g.collective_compute("AllGather", ALU.bypass, replica_groups=[list(range(8))],
                         ins=[kvh_t.ap().opt()], outs=[kvall_t.ap().opt()]).then_inc(cc_sem)
    cc_cnt[0] += 1
    g.wait_ge(cc_sem, cc_cnt[0])
    kvv = kvall.rearrange("(k r) t -> k r t", r=2 * D)
    xdma(kTi[:, 0:HALO], kvv[bass.ds(hr, 1), 0:D, :].rearrange("k r t -> (k r) t"))
    xdma(vTi[:, 0:HALO], kvv[bass.ds(hr, 1), D:2 * D, :].rearrange("k r t -> (k r) t"))
    xwait()
    sync_point()
    emit_attn(nc, c, {"qT": qkv[0:D, :], "kT": kTi, "vT": vTi, "bias": abias, "hm": hm, "idm": idm, "oT": oTi})
    c.end_phase("d_")
    emit_dense(nc, c, {"xT": x1, "gains": g_c, "yT": oTi, "w_o": w_o1, "w_g": w_g1, "w_u": w_u1, "w_d": w_d1, "xo": xo},
               NTH, True, True, 0, True, name="block1")
    c.finish()
    c.close()
    return nc


def kernel(**inputs):
    f32 = np.float32
    I = {k: np.asarray(v) for k, v in inputs.items()}
    x = I["x"].astype(f32, copy=False)
    ng = I["norm_g"]
    B, T, Dm = x.shape
    L0 = {k: I[k][0] for k in I if k.startswith("rwkv") or k.startswith("ssm")}
    w_in = I["w_in_ab"][0]
    if "nc" not in _CACHE:
        _CACHE["nc"] = build_fused()
    nc = _CACHE["nc"]
    wp = [tile_w(gather_cols(w_in, half_cols(hh))) for hh in range(2)]
    mp = [mixer_params(hh, L0) for hh in range(2)]
    sc, ssel = ssd_consts()
    shared = {
        "g_a": gains_layout([ng[0, 0]]),
        "g_b": gains_layout([ng[0, 1], ng[0, 2], ng[0, 3], ng[1, 0]]),
        "g_c": gains_layout([ng[1, 1], ng[1, 2], ng[1, 3]]),
        "rcst": rwkv_consts(), "scst": sc, "sel": ssel,
        "w_o0": tile_w(I["w_out_ab"][0]), "w_g0": tile_w(I["ffn_w_gate"][0]), "w_u0": tile_w(I["ffn_w_up"][0]),
        "w_d0": tile_w(I["ffn_w_down"][0]), "w_qkv": tile_w(I["w_qkv"][0]),
        "w_o1": tile_w(I["w_out_c"][0]), "w_g1": tile_w(I["ffn_w_gate"][1]), "w_u1": tile_w(I["ffn_w_up"][1]),
        "w_d1": tile_w(I["ffn_w_down"][1]),
        "abias": attn_bias(I["attn_rel_bias"][0]), "idm": np.eye(128, dtype=f32),
    }
    xTb = [np.ascontiguousarray(x[b].T) for b in range(B)]
    in_maps = []
    for c in range(8):
        b, s = c // 2, c % 2
        d = dict(shared)
        d["xT"] = xTb[b]
        d["xTh"] = np.ascontiguousarray(xTb[b][:, s * NTH:(s + 1) * NTH])
        d["cid"] = np.array([[2 * b, 2 * b + 1, s, (c - 1) if s == 1 else c, 0, 0, 0, 0]], np.int32)
        d["w_in"] = wp[s]
        d["par"] = mp[s]["par"]; d["hpar"] = mp[s]["hpar"]; d["lora"] = mp[s]["lora"]
        d["hm"] = halo_mask(s)
        in_maps.append(d)
    res = run_bass_kernel_spmd(nc, in_maps, core_ids=list(range(8)))
    out = np.empty((B, T, Dm), f32)
    for c in range(8):
        b, s = c // 2, c % 2
        out[b, s * NTH:(s + 1) * NTH, :] = res.results[c]["xo"].T
    return out
```
